# Optimizing a Trainium2 kernel written in Bass

```python
import math
import jax, jax.numpy as jnp
from jax import lax
import numpy as np

D_MODEL = 1024
BATCH = 8
SEQ = 4096
DEPTH = 4

N_MEM = 256
D_FF = 2816

DA_HEADS = 4
DA_HEAD_DIM = 64
DA_V_DIM = 2 * DA_HEAD_DIM
DA_QK_WIDTH = DA_HEADS * 2 * DA_HEAD_DIM
DA_WIDTH = DA_HEADS * DA_V_DIM

POOL_WINDOWS = (2, 4, 8, 16)
POOL_GROUPS = 4
POOL_WIDTH = 256
POOL_GROUP_DIM = POOL_WIDTH // POOL_GROUPS

CONV_WIDTH = 256
CONV_KERNEL = 31

XA_HEADS = 4
XA_HEAD_DIM = 64
XA_WIDTH = XA_HEADS * XA_HEAD_DIM

N_BRANCH = 3
Q_BLOCK = 128
EPS = 1e-6
NEG_INF = -1e30

IN_SPLITS = (DA_QK_WIDTH, DA_QK_WIDTH, DA_WIDTH, POOL_WIDTH, 2 * CONV_WIDTH, N_BRANCH * D_MODEL)
IN_COLS = DA_QK_WIDTH * 2 + DA_WIDTH + POOL_WIDTH + 2 * CONV_WIDTH + N_BRANCH * D_MODEL

kernel_name = 'hybrid_gated_diffattn_pool_conv_macaron'


def rmsnorm(x, g):
    xf = x.astype(jnp.float32)
    y = xf * lax.rsqrt(jnp.mean(xf * xf, axis=-1, keepdims=True) + EPS)
    return (y * g.astype(jnp.float32)).astype(x.dtype)


def layernorm(x, g, b):
    xf = x.astype(jnp.float32)
    mu = jnp.mean(xf, axis=-1, keepdims=True)
    xc = xf - mu
    y = xc * lax.rsqrt(jnp.mean(xc * xc, axis=-1, keepdims=True) + EPS)
    return (y * g.astype(jnp.float32) + b.astype(jnp.float32)).astype(x.dtype)


def swiglu_half_step(x, norm_g, w_gu, w_down):
    h = rmsnorm(x, norm_g)
    gate, up = jnp.split(h @ w_gu, 2, axis=-1)
    return x + 0.5 * ((jax.nn.silu(gate) * up) @ w_down)


def split_cols(z):
    idx, acc = [], 0
    for w in IN_SPLITS[:-1]:
        acc += w
        idx.append(acc)
    return jnp.split(z, idx, axis=-1)


def diff_attention(q, k, v, lam, lam_init, subln_g):
    b, s = q.shape[0], q.shape[1]
    nblk = s // Q_BLOCK
    qb = jnp.moveaxis(q.reshape(b, nblk, Q_BLOCK, DA_HEADS, 2, DA_HEAD_DIM), 1, 0)
    kpos = jnp.arange(s)

    def one_block(args):
        i, qi = args
        scores = jnp.einsum('bqhcd,bkhcd->bhcqk', qi, k).astype(jnp.float32)
        qpos = i * Q_BLOCK + jnp.arange(Q_BLOCK)
        causal = kpos[None, :] <= qpos[:, None]
        p = jax.nn.softmax(jnp.where(causal, scores, NEG_INF), axis=-1)
        a = p[:, :, 0] - lam * p[:, :, 1]
        return jnp.einsum('bhqk,bkhe->bqhe', a.astype(v.dtype), v)

    o = lax.map(one_block, (jnp.arange(nblk), qb))
    o = jnp.moveaxis(o, 0, 1).reshape(b, s, DA_HEADS, DA_V_DIM)
    o = rmsnorm(o, subln_g) * (1.0 - lam_init)
    return o.reshape(b, s, DA_WIDTH)


def pool_mixer(u, w_group, scale):
    b, s, _ = u.shape
    uf = u.astype(jnp.float32).reshape(b, s, POOL_GROUPS, POOL_GROUP_DIM)
    c = jnp.cumsum(uf, axis=1)
    c_pad = jnp.pad(c, ((0, 0), (1, 0), (0, 0), (0, 0)))
    pos1 = jnp.arange(1, s + 1, dtype=jnp.float32)
    outs = []
    for g, w in enumerate(POOL_WINDOWS):
        cg = c_pad[:, :, g]
        lag = jnp.pad(cg, ((0, 0), (w, 0), (0, 0)))[:, 1:s + 1]
        win_sum = cg[:, 1:] - lag
        cnt = jnp.minimum(pos1, float(w))
        outs.append(win_sum / cnt[None, :, None] - uf[:, :, g])
    p = jnp.stack(outs, axis=2).astype(u.dtype)
    y = jnp.einsum('bsgc,gcd->bsgd', p, w_group).reshape(b, s, POOL_WIDTH)
    return y * scale


def conv_module(u, dw_w, dw_b, ln_g, ln_b):
    a, gate = jnp.split(u, 2, axis=-1)
    z = a * jax.nn.sigmoid(gate)
    z = lax.conv_general_dilated(
        z, dw_w[:, None, :], window_strides=(1,), padding=[(CONV_KERNEL - 1, 0)],
        dimension_numbers=('NWC', 'WIO', 'NWC'), feature_group_count=CONV_WIDTH) + dw_b
    z = layernorm(z, ln_g, ln_b)
    return jax.nn.silu(z)


def cross_attention_step(x, mem, norm_g, mem_norm_g, w_q, w_kv, q_g, k_g, w_o):
    b, s, _ = x.shape
    m_len = mem.shape[1]
    h = rmsnorm(x, norm_g)
    m = rmsnorm(mem, mem_norm_g)
    q = rmsnorm((h @ w_q).reshape(b, s, XA_HEADS, XA_HEAD_DIM), q_g) * (XA_HEAD_DIM ** -0.5)
    k, v = jnp.split(m @ w_kv, 2, axis=-1)
    k = rmsnorm(k.reshape(b, m_len, XA_HEADS, XA_HEAD_DIM), k_g)
    v = v.reshape(b, m_len, XA_HEADS, XA_HEAD_DIM)
    p = jax.nn.softmax(jnp.einsum('bqhd,bkhd->bhqk', q, k).astype(jnp.float32), axis=-1)
    o = jnp.einsum('bhqk,bkhd->bqhd', p.astype(v.dtype), v).reshape(b, s, XA_WIDTH)
    return x + o @ w_o


def setup_inputs(seed: int = 0) -> dict:
    key = jax.random.key(seed)
    ks = jax.random.split(key, 40)
    L, D = DEPTH, D_MODEL
    f32 = jnp.float32

    def w(k, shape, fan_in):
        return jax.random.normal(k, shape, f32) * (fan_in ** -0.5)

    def gain(k, shape):
        return 1.0 + 0.02 * jax.random.normal(k, shape, f32)

    def bias(k, shape):
        return 0.02 * jax.random.normal(k, shape, f32)

    return {
        'x': jax.random.normal(ks[0], (BATCH, SEQ, D), f32),
        'mem': jax.random.normal(ks[1], (BATCH, N_MEM, D), f32),
        'ffn1_norm': gain(ks[2], (L, D)),
        'ffn1_w_gu': w(ks[3], (L, D, 2 * D_FF), D),
        'ffn1_w_down': w(ks[4], (L, D_FF, D), D_FF),
        'mix_norm': gain(ks[5], (L, D)),
        'w_in': w(ks[6], (L, D, IN_COLS), D),
        'b_gate': bias(ks[7], (L, N_BRANCH * D)),
        'da_q_norm': gain(ks[8], (L, DA_HEAD_DIM)),
        'da_k_norm': gain(ks[9], (L, DA_HEAD_DIM)),
        'da_lambda': 0.1 * jax.random.normal(ks[10], (L, 4, DA_HEAD_DIM), f32),
        'da_subln': gain(ks[11], (L, DA_V_DIM)),
        'w_proj_attn': w(ks[12], (L, DA_WIDTH, D), DA_WIDTH),
        'pool_w': w(ks[13], (L, POOL_GROUPS, POOL_GROUP_DIM, POOL_GROUP_DIM), POOL_GROUP_DIM),
        'pool_scale': gain(ks[14], (L, POOL_WIDTH)),
        'w_proj_pool': w(ks[15], (L, POOL_WIDTH, D), POOL_WIDTH),
        'conv_dw': w(ks[16], (L, CONV_KERNEL, CONV_WIDTH), CONV_KERNEL),
        'conv_db': bias(ks[17], (L, CONV_WIDTH)),
        'conv_ln_g': gain(ks[18], (L, CONV_WIDTH)),
        'conv_ln_b': bias(ks[19], (L, CONV_WIDTH)),
        'w_proj_conv': w(ks[20], (L, CONV_WIDTH, D), CONV_WIDTH),
        'w_out': w(ks[21], (L, D, D), D),
        'xa_norm': gain(ks[22], (L, D)),
        'xa_mem_norm': gain(ks[23], (L, D)),
        'xa_w_q': w(ks[24], (L, D, XA_WIDTH), D),
        'xa_w_kv': w(ks[25], (L, D, 2 * XA_WIDTH), D),
        'xa_q_norm': gain(ks[26], (L, XA_HEAD_DIM)),
        'xa_k_norm': gain(ks[27], (L, XA_HEAD_DIM)),
        'xa_w_o': w(ks[28], (L, XA_WIDTH, D), XA_WIDTH),
        'ffn2_norm': gain(ks[29], (L, D)),
        'ffn2_w_gu': w(ks[30], (L, D, 2 * D_FF), D),
        'ffn2_w_down': w(ks[31], (L, D_FF, D), D_FF),
    }


def reference(x, mem, ffn1_norm, ffn1_w_gu, ffn1_w_down, mix_norm, w_in, b_gate,
              da_q_norm, da_k_norm, da_lambda, da_subln, w_proj_attn,
              pool_w, pool_scale, w_proj_pool,
              conv_dw, conv_db, conv_ln_g, conv_ln_b, w_proj_conv,
              w_out, xa_norm, xa_mem_norm, xa_w_q, xa_w_kv, xa_q_norm, xa_k_norm, xa_w_o,
              ffn2_norm, ffn2_w_gu, ffn2_w_down):
    b, s, _ = x.shape
    for l in range(DEPTH):
        x = swiglu_half_step(x, ffn1_norm[l], ffn1_w_gu[l], ffn1_w_down[l])

        h = rmsnorm(x, mix_norm[l])
        zq, zk, zv, zp, zc, zg = split_cols(h @ w_in[l])

        q = rmsnorm(zq.reshape(b, s, DA_HEADS, 2, DA_HEAD_DIM), da_q_norm[l]) * (DA_HEAD_DIM ** -0.5)
        k = rmsnorm(zk.reshape(b, s, DA_HEADS, 2, DA_HEAD_DIM), da_k_norm[l])
        v = zv.reshape(b, s, DA_HEADS, DA_V_DIM)
        lam_init = 0.8 - 0.6 * math.exp(-0.3 * l)
        lq = da_lambda[l].astype(jnp.float32)
        lam = jnp.exp(jnp.sum(lq[0] * lq[1])) - jnp.exp(jnp.sum(lq[2] * lq[3])) + lam_init
        y_a = diff_attention(q, k, v, lam, lam_init, da_subln[l]) @ w_proj_attn[l]

        y_b = pool_mixer(zp, pool_w[l], pool_scale[l]) @ w_proj_pool[l]

        y_c = conv_module(zc, conv_dw[l], conv_db[l], conv_ln_g[l], conv_ln_b[l]) @ w_proj_conv[l]

        gates = jax.nn.sigmoid((zg + b_gate[l]).astype(jnp.float32)).astype(x.dtype)
        gates = gates.reshape(b, s, N_BRANCH, D_MODEL)
        merged = gates[:, :, 0] * y_a + gates[:, :, 1] * y_b + gates[:, :, 2] * y_c
        x = x + merged @ w_out[l]

        x = cross_attention_step(x, mem, xa_norm[l], xa_mem_norm[l], xa_w_q[l], xa_w_kv[l],
                                 xa_q_norm[l], xa_k_norm[l], xa_w_o[l])

        x = swiglu_half_step(x, ffn2_norm[l], ffn2_w_gu[l], ffn2_w_down[l])
    return x
```

```python
from contextlib import ExitStack
import math
import numpy as np
import concourse.bass as bass
import concourse.mybir as mybir
from concourse.bass_utils import run_bass_kernel_spmd

F32 = mybir.dt.float32
BF16 = mybir.dt.bfloat16
AF = mybir.ActivationFunctionType
ALU = mybir.AluOpType

D = 1024
S = 4096
DEPTH = 4
NMEM = 256
DFF = 2816
NCH = 8
FCH = 22
T = 512
NT = S // T
EPS = 1e-6
INC = 5376
NCORES = 8


class Buf:
    __slots__ = ("name", "w", "r", "dsem", "dcount")

    def __init__(self, name, dsem=None):
        self.name = name
        self.w = {}
        self.r = {}
        self.dsem = dsem
        self.dcount = 0


class Op:
    __slots__ = ("eng", "fn", "waits", "sig", "tok", "idx")


class Eng:
    def __init__(self, name):
        self.name = name
        self.ops = []
        self.count = 0
        self.pending = []
        self.waited = {}
        self.semkey = "e_" + name


class Prog:
    def __init__(self):
        self.engs = {n: Eng(n) for n in ("pe", "act", "dve", "pool", "sp")}
        self.semkeys = [e.semkey for e in self.engs.values()]
        self.nbuf = 0

    def buf(self, name, dma=False):
        self.nbuf += 1
        dsem = None
        if dma:
            dsem = "d_" + name
            assert dsem not in self.semkeys
            self.semkeys.append(dsem)
        return Buf(name, dsem)

    def op(self, eng, fn, reads=(), writes=(), signal=True, dma=None):
        E = self.engs[eng]
        o = Op()
        o.eng = E
        o.fn = fn
        o.tok = None
        o.sig = None
        deps = []
        for b in reads:
            deps.extend(b.w.values())
        for b in writes:
            deps.extend(b.w.values())
            deps.extend(b.r.values())
        waits = []
        for d in deps:
            if d.eng is E and eng == "pe" and d.tok is None:
                continue
            if d.eng is E and eng == "pe" and d.tok[0] == E.semkey:
                continue
            assert d.tok is not None, "dependency on unsignalled op"
            k, v = d.tok
            if E.waited.get(k, 0) < v:
                E.waited[k] = v
        o.waits = waits
        o.idx = len(E.ops)
        E.ops.append(o)
        if dma is not None:
            dma.dcount += 16
            o.tok = (dma.dsem, dma.dcount)
            o.sig = (dma.dsem, 16)
        elif signal:
            E.count += 1
            o.tok = (E.semkey, E.count)
            o.sig = (E.semkey, 1)
            for p in E.pending:
                p.tok = o.tok
            E.pending = []
        else:
            E.pending.append(o)
        for b in reads:
            b.r[o.tok[0] if o.tok else ("pend", E.name)] = o
        for b in writes:
            b.w = {o.tok[0] if o.tok else ("pend", E.name): o}
            b.r = {}
        return o


class Prog2(Prog):
    def op(self, eng, fn, reads=(), writes=(), signal=True, dma=None):
        E = self.engs[eng]
        before = dict(E.waited)
        o = Prog.op(self, eng, fn, reads, writes, signal, dma)
        o.waits = [(k, v) for k, v in E.waited.items() if before.get(k, 0) < v]
        return o

    def check(self):
        sem = {k: 0 for k in self.semkeys}
        pos = {n: 0 for n in self.engs}
        progress = True
        while progress:
            progress = False
            for n, E in self.engs.items():
                while pos[n] < len(E.ops):
                    o = E.ops[pos[n]]
                    if all(sem[k] >= v for k, v in o.waits):
                        if o.sig:
                            sem[o.sig[0]] += o.sig[1]
                        pos[n] += 1
                        progress = True
                    else:
                        break
        for n, E in self.engs.items():
            if pos[n] < len(E.ops):
                o = E.ops[pos[n]]
                raise RuntimeError(
                    f"deadlock: engine {n} stuck at op {pos[n]}/{len(E.ops)} waits={o.waits} "
                    f"sems={ {k: sem[k] for k, _ in o.waits} }")

    def emit(self, nc, stack):
        for E in self.engs.values():
            assert not E.pending or E.name == "pe", E.name
        sems = {k: stack.enter_context(nc.semaphore(k)) for k in self.semkeys}
        engs = self.engs

        def run(name):
            def f(e):
                fold = name in ("pe", "act", "dve")
                for o in engs[name].ops:
                    waits = list(o.waits)
                    last = waits.pop() if (fold and waits and o.fn is not None) else None
                    for k, v in waits:
                        e.wait_ge(sems[k], v)
                    if o.fn is None:
                        continue
                    ins = o.fn(e)
                    if last is not None:
                        ins._wait_ge(sems[last[0]], last[1])
                    if o.sig:
                        ins.then_inc(sems[o.sig[0]], o.sig[1])
            return f

        with nc.Block() as block:
            block.tensor(run("pe"))
            block.scalar(run("act"))
            block.vector(run("dve"))
            block.gpsimd(run("pool"))
            block.sync(run("sp"))


SP_FFN1 = 0
SP_MIX = 8
SP_XA = 16
SP_MEMN = 24
SP_FFN2 = 32
SP_BGATE = 40
SP_DAQ = 64
SP_DAK = 65
SP_SUBLN = 66
SP_PSCALE = 67
SP_CDB = 69
SP_CLNG = 71
SP_CLNB = 73
SP_XAQ = 75
SP_XAK = 76
SP_CDW = 77
NSP = 139

C_ONES = 0
C_BLK64 = 128
C_TRI = 256
C_RW = 384
C_RC0 = 386
NCONST = 418

WSLOT_ELEMS = 4096
NWSLOT = 3
PH = 16
CHL = 32


def build(layers=(0, 1, 2, 3), ntiles=NT, stages=("ffn1", "mix", "xa", "ffn2")):
    nc = bass.Bass("TRN2", target_bir_lowering=False)
    L = DEPTH

    def din(name, shape, dt=F32):
        return nc.dram_tensor(name, list(shape), dt, kind="ExternalInput").ap()

    def dint(name, shape, dt=BF16):
        return nc.dram_tensor(name, list(shape), dt, kind="Internal").ap()

    xT = din("xT", [D, S])
    memT = din("memT", [D, NMEM])
    spar = din("spar", [L, 128, NSP])
    lamb = din("lamb", [L, 128, 256])
    consts = din("consts", [128, NCONST])
    w_gu = [din("ffn1_w_gu", [L, D, 2 * DFF]), din("ffn2_w_gu", [L, D, 2 * DFF])]
    w_dn = [din("ffn1_w_down", [L, DFF, D]), din("ffn2_w_down", [L, DFF, D])]
    w_in = din("w_in", [L, D, INC])
    w_pa = din("w_proj_attn", [L, 512, D])
    w_pp = din("w_proj_pool", [L, 256, D])
    w_pc = din("w_proj_conv", [L, 256, D])
    w_out = din("w_out", [L, D, D])
    pool_w = din("pool_w", [L, 4, 64, 64])
    xa_wq = din("xa_w_q", [L, D, 256])
    xa_wkv = din("xa_w_kv", [L, D, 512])
    xa_wo = din("xa_w_o", [L, 256, D])
    yT = nc.dram_tensor("yT", [D, S], F32, kind="ExternalOutput").ap()

    s_gu = [dint(f"s_gu{f}", [L, 11, 128, 2, 8, 256]) for f in range(2)]
    s_dn = [dint(f"s_dn{f}", [L, 8, 128, 22, 128]) for f in range(2)]
    s_qkvc = dint("s_qkvc", [L, 4, 128, 8, 512])
    s_pool = dint("s_pool", [L, 128, 8, 256])
    s_mrg = dint("s_mrg", [L, 8, 128, 8, 512])
    s_wout = dint("s_wout", [L, 2, 128, 8, 512])
    s_xaq = dint("s_xaq", [L, 128, 8, 256])
    s_xakv = dint("s_xakv", [L, 128, 8, 512])
    s_xao = dint("s_xao", [L, 128, 2, 1024])
    kcache = dint("kcache", [L, 4, 128, S])
    vcache = dint("vcache", [L, 4, 128, S // 128, 128])

    P = Prog2()
    st = ExitStack()

    def sb(name, shape, dt):
        return st.enter_context(nc.sbuf_tensor(name, list(shape), dt))

    x_t = sb("x_t", [128, NCH, T], F32)
    h_t = sb("h_t", [128, NCH, T], BF16)
    a_raw = sb("a_t", [128, FCH * T], BF16)
    a_t = a_raw[:, :].rearrange("p (c t) -> p c t", c=FCH, t=T)
    KPRE = 3584
    kpre = [a_raw[:, 0:KPRE], a_raw[:, KPRE:2 * KPRE]]
    vpre0 = a_raw[:, 2 * KPRE:3 * KPRE].rearrange("p (k e) -> p k e", k=28, e=128)
    vpre1_raw = sb("vpre1", [128, 28, 128], BF16)
    vpre = [vpre0, vpre1_raw[:, :, :]]
    wsl = [sb(f"wsl{i}", [128, WSLOT_ELEMS], BF16) for i in range(NWSLOT)]
    NSTG = 3
    stg_t = [sb(f"stg{i}", [128, 2048], F32) for i in range(NSTG)]
    cst = sb("cst", [128, NCONST], F32)
    cstb = sb("cstb", [128, 384], BF16)
    onesd = sb("onesd", [128, 4, 128], BF16)
    ones256f = sb("ones256f", [128, 128], F32)
    epst = sb("epst", [128, 8], F32)
    junk = sb("junk", [128, 8], F32)
    spt = sb("spt", [128, L, NSP], F32)
    neglam = sb("neglam", [128, L], F32)
    NSQ = 3
    sq_t = [sb(f"sq{i}", [128, T], BF16) for i in range(NSQ)]
    NTMP = 6
    tmp_t = [sb(f"tmp{i}", [128, T], F32) for i in range(NTMP)]
    ps_t = [st.enter_context(nc.psum_tensor(f"ps{i}", [128, T], F32)) for i in range(8)]
    q_t = sb("q_t", [128, 2, 4, T], BF16)
    kcur = sb("kcur", [128, 4, T], BF16)
    vcur = sb("vcur", [128, 4, T], BF16)
    NPT = 6
    pt_t = [sb(f"pt{i}", [128, T], BF16) for i in range(NPT)]
    pbuf = sb("pbuf", [128, 2, PH + T], F32)
    wA = sb("wA", [128, 2, PH + T], F32)
    wB = sb("wB", [128, 2, PH + T], F32)
    pp_t = sb("pp_t", [128, 2, T], BF16)
    ypool = sb("ypool", [128, 2, T], BF16)
    cbuf = sb("cbuf", [128, 2, CHL + T], F32)
    cacc = sb("cacc", [128, 2, T], F32)
    yconv = sb("yconv", [128, 2, T], BF16)
    oa_t = sb("oa_t", [128, 4, T], BF16)
    mrg_t = sb("mrg_t", [128, NCH, T], BF16)
    qx_t = sb("qx_t", [128, 4, T], BF16)
    ox_t = sb("ox_t", [128, 2, T], BF16)
    kmem = sb("kmem", [128, L, 2, NMEM], BF16)
    vmem = sb("vmem", [128, L, 2, NMEM], BF16)
    pwbd = sb("pwbd", [128, L, 2, 128], BF16)
    phalo = sb("phalo", [128, L, 2, PH], F32)
    chalo = sb("chalo", [128, L, 2, CHL], F32)

    B = P.buf
    xb = [B(f"x{c}") for c in range(NCH)]
    x_dma = B("xdma", dma=True)
    hb = [B(f"h{c}") for c in range(NCH)]
    ab = [B(f"a{c}") for c in range(FCH)]
    kpre_d = [B("kpre0", dma=True), B("kpre1", dma=True)]
    vpre_d = [B("vpre0", dma=True), B("vpre1", dma=True)]
    kpreb = [[kpre_d[0]] + ab[0:7], [kpre_d[1]] + ab[7:14]]
    vpreb = [[vpre_d[0]] + ab[14:21], [vpre_d[1]]]
    wslb = [B(f"wsl{i}", dma=True) for i in range(NWSLOT)]
    stgb = [B(f"stg{i}", dma=True) for i in range(NSTG)]
    cst_b = B("cst", dma=True)
    cstb_b = B("cstb")
    spt_b = B("spt", dma=True)
    lam_b = B("lam")
    junkb = B("junk")
    sqb = [B(f"sq{i}") for i in range(NSQ)]
    tmpb = [B(f"tmp{i}", dma=True) for i in range(NTMP)]
    psb = [B(f"ps{i}") for i in range(8)]
    qb = [B(f"q{h}") for h in range(4)]
    kcurb = B("kcur", dma=True)
    kcurh = [B(f"kcur{h}") for h in range(4)]
    vcurb = B("vcur", dma=True)
    ptb = [B(f"pt{i}") for i in range(NPT)]
    pbufb = B("pbuf")
    wAb = [B("wA0"), B("wA1")]
    wBb = [B("wB0"), B("wB1")]
    ppb = [B("pp0"), B("pp1")]
    ypoolb = [B("ypool0"), B("ypool1")]
    cbufb = [B("cbuf0"), B("cbuf1")]
    caccb = [B("cacc0"), B("cacc1")]
    ctmpb = B("ctmp")
    yconvb = [B("yconv0"), B("yconv1")]
    oab = [B(f"oa{h}") for h in range(4)]
    daccb = [B(f"dacc{k}") for k in range(4)]
    mrgb = [B(f"mrg{c}") for c in range(NCH)]
    qxb = [B(f"qx{h}") for h in range(4)]
    oxb = [B(f"ox{h}") for h in range(4)]
    kmemb = B("kmem")
    vmemb = B("vmem")
    pwbdb = B("pwbd", dma=True)
    phalob = B("phalo")
    chalob = B("chalo")
    kcacheb = [B(f"kcache{l}") for l in range(L)]
    vcacheb = [B(f"vcache{l}") for l in range(L)]
    conv_b = {}

    ctr = {"ws": 0, "sq": 0, "tmp": 0, "ps": 0, "pt": 0, "pss": 0, "stg": 0, "cast": 0}
    ps_limit = [8]

    def nxt(kind, n):
        i = ctr[kind] % n
        ctr[kind] += 1
        return i

    held = set()

    def nps():
        while True:
            b = nxt("ps", ps_limit[0])
            if b not in held:
                return b

    def sp_col(l, col, n=1):
        return spt[:, l, col:col + n]

    P.op("sp", lambda e: e.dma_start(out=cst[:], in_=consts), writes=[cst_b], dma=cst_b)
    P.op("sp", lambda e: e.dma_start(out=spt[:], in_=spar.rearrange("l p n -> p l n")),
         writes=[spt_b], dma=spt_b)
    P.op("dve", lambda e: e.tensor_copy(out=cstb[:], in_=cst[:, 0:384]), reads=[cst_b], writes=[cstb_b])
    for k, (c0, val) in enumerate(((C_ONES, 1.0 / D), (C_BLK64, 1.0 / 64), (C_ONES, 1.0 / 128), (C_ONES, 1.0 / 256))):
        P.op("dve", lambda e, k=k, c0=c0, val=val: e.tensor_scalar(out=onesd[:, k, :], in0=cst[:, c0:c0 + 128],
                                                                   scalar1=val, scalar2=None, op0=ALU.mult),
             reads=[cst_b], writes=[cstb_b])
    P.op("dve", lambda e: e.tensor_scalar(out=ones256f[:], in0=cst[:, C_ONES:C_ONES + 128], scalar1=1.0 / 256,
                                          scalar2=None, op0=ALU.mult), reads=[cst_b], writes=[cstb_b])
    eps_vals = [EPS] + [EPS / (1.0 - (0.8 - 0.6 * math.exp(-0.3 * l))) ** 2 for l in range(L)]
    epsb = {}
    for k, v in enumerate(eps_vals):
        epsb[v] = epst[:, k:k + 1]
        P.op("pool", lambda e, k=k, v=v: e.memset(epst[:, k:k + 1], v), writes=[cstb_b])

    P.op("pool", lambda e: e.memset(q_t[:], 0.0), writes=qb)
    P.op("pool", lambda e: e.memset(qx_t[:], 0.0), writes=qxb)

    def conv_op(key, out_ap, in_ap):
        b = conv_b.get(key)
        if b is None:
            b = conv_b[key] = B("cv_%s_%d" % key, dma=True)
        P.op("pool", lambda e: e.dma_start(out=out_ap, in_=in_ap), writes=[b], dma=b)

    def convert_ffn(f, l):
        for s in range(11):
            src = w_gu[f][l].rearrange("(c p) (g s f) -> s p g c f", c=8, p=128, g=2, s=11, f=256)[s]
            conv_op(("gu%d" % f, l), s_gu[f][l, s], src)
        for s in range(8):
            src = w_dn[f][l].rearrange("(c p) (s f) -> s p c f", c=22, p=128, s=8, f=128)[s]
            conv_op(("dn%d" % f, l), s_dn[f][l, s], src)

    def kc(ap):
        return ap.rearrange("(c p) f -> p c f", p=128)

    def convert_mix(l):
        for s, c0 in enumerate((0, 512, 1024, 1792)):
            conv_op(("qkvc", l), s_qkvc[l, s], kc(w_in[l][:, c0:c0 + 512]))
        conv_op(("qkvc", l), s_pool[l], kc(w_in[l][:, 1536:1792]))
        for m in range(8):
            ms = slice(m * 128, (m + 1) * 128)
            conv_op(("mrg", l), s_mrg[l, m][:, 0:4, 0:128], kc(w_pa[l][:, ms]))
            conv_op(("mrg", l), s_mrg[l, m][:, 4:6, 0:128], kc(w_pp[l][:, ms]))
            conv_op(("mrg", l), s_mrg[l, m][:, 6:8, 0:128], kc(w_pc[l][:, ms]))
            for g in range(3):
                c0 = 2304 + g * 1024 + m * 128
                conv_op(("mrg", l), s_mrg[l, m][:, :, 128 + g * 128:256 + g * 128], kc(w_in[l][:, c0:c0 + 128]))
        for s in range(2):
            conv_op(("mrg", l), s_wout[l, s], kc(w_out[l][:, s * 512:(s + 1) * 512]))

    def convert_xa(l):
        conv_op(("xa", l), s_xaq[l], kc(xa_wq[l]))
        conv_op(("xa", l), s_xakv[l], kc(xa_wkv[l]))
        conv_op(("xa", l), s_xao[l], kc(xa_wo[l]))

    def wload(src_ap, view, key, nparts=128):
        i = nxt("ws", NWSLOT)
        n = 1
        for d in src_ap.shape[1:]:
            n *= d
        assert n <= WSLOT_ELEMS, n
        dst_flat = wsl[i][0:nparts, 0:n]
        src_flat = src_ap
        if len(src_ap.shape) > 2:
            names = " ".join("d%d" % k for k in range(len(src_ap.shape) - 1))
            src_flat = src_ap.rearrange("p %s -> p (%s)" % (names, names))
        P.op("sp", lambda e: e.dma_start(out=dst_flat, in_=src_flat), reads=[conv_b[key]],
             writes=[wslb[i]], dma=wslb[i])
        return view(wsl[i]), wslb[i]

    def wload_direct(pieces, n, scratch_slab, view, key):
        i = nxt("ws", NWSLOT)
        b = conv_b.get(key)
        if b is None:
            b = conv_b[key] = B("cv_%s_%d" % key, dma=True)
        for src, off in pieces:
            a_, b_ = src.shape[1], src.shape[2]
            ne = a_ * b_
            u = nxt("stg", NSTG)
            sview = stg_t[u][:, 0:ne].rearrange("p (a b) -> p a b", a=a_, b=b_)
            P.op("sp", lambda e, sview=sview, src=src: e.dma_start(out=sview, in_=src), writes=[stgb[u]], dma=stgb[u])
            eng = "act" if nxt("cast", 2) == 0 else "dve"
            if eng == "act":
                P.op("act", lambda e, u=u, off=off, ne=ne, i=i: e.activation(out=wsl[i][:, off:off + ne], in_=stg_t[u][:, 0:ne], func=AF.Copy),
                     reads=[stgb[u]], writes=[wslb[i]])
            else:
                P.op("dve", lambda e, u=u, off=off, ne=ne, i=i: e.tensor_copy(out=wsl[i][:, off:off + ne], in_=stg_t[u][:, 0:ne]),
                     reads=[stgb[u]], writes=[wslb[i]])
        flat = scratch_slab
        if len(flat.shape) > 2:
            names = " ".join("d%d" % k for k in range(len(flat.shape) - 1))
            flat = flat.rearrange("p %s -> p (%s)" % (names, names))
        P.op("sp", lambda e, i=i, flat=flat: e.dma_start(out=flat, in_=wsl[i][:, 0:n]), reads=[wslb[i]], writes=[b], dma=b)
        return view(wsl[i]), wslb[i]

    def v3(a, b):
        return lambda t: t[:, 0:a * b].rearrange("p (c f) -> p c f", c=a, f=b)

    def rstd_from_ps(pi, eps, n=T):
        ti = nxt("tmp", NTMP)
        P.op("act", lambda e, ti=ti, pi=pi: e.activation(out=tmp_t[ti][:, 0:n], in_=ps_t[pi][:, 0:n], func=AF.Ln,
                                                        bias=epsb[eps][:, 0:1]),
             reads=[psb[pi], cstb_b], writes=[tmpb[ti]])
        P.op("act", lambda e, ti=ti: e.activation(out=tmp_t[ti][:, 0:n], in_=tmp_t[ti][:, 0:n], func=AF.Exp, scale=-0.5),
             reads=[tmpb[ti]], writes=[tmpb[ti]])
        return ti

    def recip_from_ps(pi, n=T):
        ti = nxt("tmp", NTMP)
        P.op("act", lambda e, ti=ti, pi=pi: e.activation(out=tmp_t[ti][:, 0:n], in_=ps_t[pi][:, 0:n], func=AF.Ln),
             reads=[psb[pi]], writes=[tmpb[ti]])
        P.op("act", lambda e, ti=ti: e.activation(out=tmp_t[ti][:, 0:n], in_=tmp_t[ti][:, 0:n], func=AF.Exp, scale=-1.0),
             reads=[tmpb[ti]], writes=[tmpb[ti]])
        return ti

    def rmsnorm_h(l, gcol, n=T):
        pi = nps()
        for c in range(NCH):
            si = nxt("sq", NSQ)
            P.op("act", lambda e, c=c, si=si: e.activation(out=sq_t[si][:, 0:n], in_=x_t[:, c, 0:n], func=AF.Square),
                 reads=[xb[c]], writes=[sqb[si]])
            P.op("pe", lambda e, c=c, si=si, pi=pi: e.matmul(ps_t[pi][:, 0:n], onesd[:, 0, :], sq_t[si][:, 0:n],
                                                            start=(c == 0), stop=(c == NCH - 1)),
                 reads=[sqb[si], cstb_b], writes=[psb[pi]], signal=True)
        ti = rstd_from_ps(pi, EPS, n)
        for c in range(NCH):
            P.op("dve", lambda e, c=c, ti=ti: e.scalar_tensor_tensor(out=h_t[:, c, 0:n], in0=x_t[:, c, 0:n],
                                                                     scalar=sp_col(l, gcol + c), in1=tmp_t[ti][:, 0:n],
                                                                     op0=ALU.mult, op1=ALU.mult),
                 reads=[xb[c], tmpb[ti], spt_b], writes=[hb[c]])

    def proj8(pi, wt, wb, col0, n=T, ncols=128):
        for c in range(NCH):
            P.op("pe", lambda e, c=c: e.matmul(ps_t[pi][0:ncols, 0:n], wt[:, c, col0:col0 + ncols], h_t[:, c, 0:n],
                                               start=(c == 0), stop=(c == NCH - 1)),
                 reads=[wb, hb[c]], writes=[psb[pi]], signal=(c == NCH - 1))

    def group_norm64(l, pz, out_ap, gcol, out_bufs, n=T):
        si = nxt("sq", NSQ)
        P.op("act", lambda e: e.activation(out=sq_t[si][:, 0:n], in_=ps_t[pz][:, 0:n], func=AF.Square),
             reads=[psb[pz]], writes=[sqb[si]])
        was_held = pz in held
        held.add(pz)
        p2 = nps()
        if not was_held:
            held.discard(pz)
        P.op("pe", lambda e: e.matmul(ps_t[p2][:, 0:n], onesd[:, 1, :], sq_t[si][:, 0:n], start=True, stop=True),
             reads=[sqb[si], cstb_b], writes=[psb[p2]])
        ti = rstd_from_ps(p2, EPS, n)
        if isinstance(out_ap, (list, tuple)):
            for half, oap in enumerate(out_ap):
                hs = slice(half * 64, (half + 1) * 64)
                P.op("dve", lambda e, hs=hs, oap=oap: e.scalar_tensor_tensor(
                    out=oap, in0=ps_t[pz][hs, 0:n], scalar=spt[hs, l, gcol:gcol + 1],
                    in1=tmp_t[ti][hs, 0:n], op0=ALU.mult, op1=ALU.mult),
                    reads=[psb[pz], tmpb[ti], spt_b], writes=[out_bufs[half]] if len(out_bufs) == 2 else out_bufs)
        else:
            P.op("dve", lambda e: e.scalar_tensor_tensor(out=out_ap, in0=ps_t[pz][:, 0:n], scalar=sp_col(l, gcol),
                                                         in1=tmp_t[ti][:, 0:n], op0=ALU.mult, op1=ALU.mult),
                 reads=[psb[pz], tmpb[ti], spt_b], writes=out_bufs)

    def lagged(items, lag=2):
        for k in range(len(items) + lag):
            if k < len(items):
                items[k][0]()
            if k - lag >= 0:
                items[k - lag][1]()

    def ffn(f, l, direct=False):
        rmsnorm_h(l, SP_FFN1 if f == 0 else SP_FFN2)
        for s in range(11):
            guview = lambda t: t[:, 0:4096].rearrange("p (g c f) -> p g c f", g=2, c=8, f=256)
            if direct:
                pieces = [(kc(w_gu[f][l][:, g * DFF + s * 256:g * DFF + (s + 1) * 256]), g * 2048) for g in range(2)]
                wt, wb = wload_direct(pieces, 4096, s_gu[f][l, s], guview, ("gu%d" % f, l))
            else:
                wt, wb = wload(s_gu[f][l, s], guview, ("gu%d" % f, l))
            banks = [(nps(), nps()) for _ in range(2)]
            if s == 0:
                for c in range(NCH):
                    for jj in range(2):
                        for g in range(2):
                            pi = banks[jj][g]
                            P.op("pe", lambda e, c=c, jj=jj, g=g, pi=pi, wt=wt: e.matmul(
                                ps_t[pi][:], wt[:, g, c, jj * 128:(jj + 1) * 128], h_t[:, c, :],
                                start=(c == 0), stop=(c == NCH - 1)),
                                reads=[wb, hb[c]], writes=[psb[pi]], signal=(c == NCH - 1))
            for jj in range(2):
                j = 2 * s + jj
                pg, pu = banks[jj]
                if s > 0:
                    proj8(pg, wt[:, 0], wb, jj * 128)
                    proj8(pu, wt[:, 1], wb, jj * 128)
                ti = nxt("tmp", NTMP)
                P.op("act", lambda e, ti=ti, pg=pg: e.activation(out=tmp_t[ti][:], in_=ps_t[pg][:], func=AF.Silu),
                     reads=[psb[pg]], writes=[tmpb[ti]])
                P.op("dve", lambda e, ti=ti, pu=pu, j=j: e.tensor_tensor(out=a_t[:, j, :], in0=ps_t[pu][:],
                                                                         in1=tmp_t[ti][:], op=ALU.mult),
                     reads=[psb[pu], tmpb[ti]], writes=[ab[j]])
        P.op("act", lambda e: e.activation(out=junk[:, 0:1], in_=epst[:, 0:1], func=AF.Ln), reads=[cstb_b], writes=[junkb])
        for m in range(8):
            if direct:
                srcm = kc(w_dn[f][l][:, m * 128:(m + 1) * 128])
                pieces = [(srcm[:, 11 * hh:11 * hh + 11, :], hh * 1408) for hh in range(2)]
                wt, wb = wload_direct(pieces, 2816, s_dn[f][l, m], v3(22, 128), ("dn%d" % f, l))
            else:
                wt, wb = wload(s_dn[f][l, m], v3(22, 128), ("dn%d" % f, l))
            po = nps()
            for c in range(FCH):
                P.op("pe", lambda e, c=c, po=po, wt=wt: e.matmul(
                    ps_t[po][:], wt[:, c, :], a_t[:, c, :], start=(c == 0), stop=(c == FCH - 1)),
                    reads=[wb, ab[c]], writes=[psb[po]], signal=(c == FCH - 1))
            P.op("dve", lambda e, m=m, po=po: e.scalar_tensor_tensor(
                out=x_t[:, m, :], in0=ps_t[po][:], scalar=0.5, in1=x_t[:, m, :], op0=ALU.mult, op1=ALU.add),
                reads=[psb[po], xb[m]], writes=[xb[m]])

    def mixer(i, l):
        t0 = i * T
        lam_init = 0.8 - 0.6 * math.exp(-0.3 * l)

        def kv_prefix_load(hd):
            slot = hd % 2
            if i > 0:
                P.op("sp", lambda e: e.dma_start(out=kpre[slot][:, 0:T * i], in_=kcache[l, hd][:, 0:T * i]),
                     reads=[kcacheb[l]], writes=kpreb[slot], dma=kpre_d[slot])
                P.op("sp", lambda e: e.dma_start(out=vpre[slot][:, 0:4 * i, :], in_=vcache[l, hd][:, 0:4 * i, :]),
                     reads=[vcacheb[l]], writes=vpreb[slot], dma=vpre_d[slot])
        rmsnorm_h(l, SP_MIX)
        wtq, wbq = wload(s_qkvc[l, 0], v3(8, 512), ("qkvc", l))
        wtk, wbk = wload(s_qkvc[l, 1], v3(8, 512), ("qkvc", l))
        pzq = []
        for hd in range(4):
            pzq.append(nps())
            held.add(pzq[hd])
        for c in range(NCH):
            for hd in range(4):
                P.op("pe", lambda e, c=c, hd=hd: e.matmul(ps_t[pzq[hd]][:], wtq[:, c, hd * 128:(hd + 1) * 128], h_t[:, c, :],
                                                          start=(c == 0), stop=(c == NCH - 1)),
                     reads=[wbq, hb[c]], writes=[psb[pzq[hd]]], signal=(c == NCH - 1))
        pzk = []
        for hd in range(4):
            pzk.append(nps())
            held.add(pzk[hd])
            proj8(pzk[hd], wtk, wbk, hd * 128)
            group_norm64(l, pzq[hd], [q_t[0:64, 0, hd, :], q_t[64:128, 1, hd, :]], SP_DAQ, [qb[hd]])
            held.discard(pzq[hd])
        for hd in range(4):
            group_norm64(l, pzk[hd], kcur[:, hd, :], SP_DAK, [kcurh[hd]])
            held.discard(pzk[hd])
        if i < NT - 1:
            P.op("pool", lambda e: e.dma_start(out=kcache[l][:, :, t0:t0 + T].rearrange("h p t -> p h t"), in_=kcur[:]),
                 reads=kcurh + [kcurb], writes=[kcacheb[l], kcurb], dma=kcurb)
        wt, wb = wload(s_qkvc[l, 2], v3(8, 512), ("qkvc", l))
        for tt in range(4):
            pv = nps()
            for c in range(NCH):
                P.op("pe", lambda e, c=c, pv=pv, tt=tt, wt=wt: e.matmul(
                    ps_t[pv][:], h_t[:, c, tt * 128:(tt + 1) * 128], wt[:, c, :], start=(c == 0), stop=(c == NCH - 1)),
                    reads=[wb, hb[c]], writes=[psb[pv]], signal=(c == NCH - 1))
            P.op("act", lambda e, pv=pv, tt=tt: e.activation(out=vcur[:, tt, :], in_=ps_t[pv][:], func=AF.Copy),
                 reads=[psb[pv]], writes=[vcurb])
        if i < NT - 1:
            for hd in range(4):
                P.op("pool", lambda e, hd=hd: e.dma_start(out=vcache[l, hd][:, 4 * i:4 * i + 4, :],
                                                          in_=vcur[:, :, hd * 128:(hd + 1) * 128]),
                     reads=[vcurb], writes=[vcacheb[l]], dma=vcurb)
        wt, wb = wload(s_pool[l], v3(8, 256), ("qkvc", l))
        if i == 0:
            P.op("pool", lambda e: e.memset(pbuf[:, :, 0:PH], 0.0), writes=[pbufb])
        else:
            P.op("pool", lambda e: e.tensor_copy(out=pbuf[:, :, 0:PH], in_=phalo[:, l, :, :]),
                 reads=[phalob], writes=[pbufb])
        for ch in range(2):
            pz = nps()
            proj8(pz, wt, wb, ch * 128)
            P.op("act", lambda e, pz=pz, ch=ch: e.activation(out=pbuf[:, ch, PH:PH + T], in_=ps_t[pz][:], func=AF.Copy),
                 reads=[psb[pz]], writes=[pbufb])
        W = PH + T
        P.op("pool", lambda e: e.tensor_tensor(out=wA[:, :, 1:W], in0=pbuf[:, :, 1:W], in1=pbuf[:, :, 0:W - 1], op=ALU.add),
             reads=[pbufb], writes=wAb)
        P.op("pool", lambda e: e.tensor_tensor(out=wB[:, :, 3:W], in0=wA[:, :, 3:W], in1=wA[:, :, 1:W - 2], op=ALU.add),
             reads=wAb, writes=wBb)
        P.op("pool", lambda e: e.tensor_tensor(out=wA[:, 1, 7:W], in0=wB[:, 1, 7:W], in1=wB[:, 1, 3:W - 4], op=ALU.add),
             reads=[wBb[1]], writes=[wAb[1]])
        P.op("pool", lambda e: e.tensor_tensor(out=wB[:, 1, 15:W], in0=wA[:, 1, 15:W], in1=wA[:, 1, 7:W - 8], op=ALU.add),
             reads=[wAb[1]], writes=[wBb[1]])
        P.op("pool", lambda e: e.tensor_copy(out=phalo[:, l, :, :], in_=pbuf[:, :, T:T + PH]),
             reads=[pbufb], writes=[phalob])
        for ch in range(2):
            for half in range(2):
                src = wA if half == 0 else wB
                srcb = wAb[ch] if half == 0 else wBb[ch]
                ps_ = slice(half * 64, (half + 1) * 64)
                P.op("dve", lambda e, ch=ch, src=src, ps_=ps_: e.scalar_tensor_tensor(
                    out=pp_t[ps_, ch, :], in0=src[ps_, ch, PH:W], scalar=cst[ps_, C_RW + ch:C_RW + ch + 1],
                    in1=pbuf[ps_, ch, PH:W], op0=ALU.mult, op1=ALU.subtract),
                    reads=[srcb, pbufb, cst_b], writes=[ppb[ch]])
                if i == 0:
                    ti = nxt("tmp", NTMP)
                    P.op("dve", lambda e, ch=ch, src=src, ps_=ps_, ti=ti: e.tensor_tensor(
                        out=tmp_t[ti][ps_, 0:16], in0=src[ps_, ch, PH:PH + 16],
                        in1=cst[ps_, C_RC0 + ch * 16:C_RC0 + (ch + 1) * 16], op=ALU.mult),
                        reads=[srcb, cst_b], writes=[tmpb[ti]])
                    P.op("dve", lambda e, ch=ch, ps_=ps_, ti=ti: e.tensor_tensor(
                        out=pp_t[ps_, ch, 0:16], in0=tmp_t[ti][ps_, 0:16], in1=pbuf[ps_, ch, PH:PH + 16], op=ALU.subtract),
                        reads=[tmpb[ti], pbufb], writes=[ppb[ch]])
        wt, wb = wload(s_qkvc[l, 3], v3(8, 512), ("qkvc", l))
        kv_prefix_load(0)
        kv_prefix_load(1)
        if i == 0:
            P.op("pool", lambda e: e.memset(cbuf[:, :, 0:CHL], 0.0), writes=cbufb)
        else:
            P.op("pool", lambda e: e.tensor_copy(out=cbuf[:, :, 0:CHL], in_=chalo[:, l, :, :]),
                 reads=[chalob], writes=cbufb)
        for ch in range(2):
            pa = nps()
            pg = nps()
            proj8(pa, wt, wb, ch * 128)
            proj8(pg, wt, wb, (2 + ch) * 128)
            ti = nxt("tmp", NTMP)
            P.op("act", lambda e, ti=ti, pg=pg: e.activation(out=tmp_t[ti][:], in_=ps_t[pg][:], func=AF.Sigmoid),
                 reads=[psb[pg]], writes=[tmpb[ti]])
            P.op("dve", lambda e, ti=ti, pa=pa, ch=ch: e.tensor_tensor(out=cbuf[:, ch, CHL:CHL + T], in0=ps_t[pa][:],
                                                                       in1=tmp_t[ti][:], op=ALU.mult),
                 reads=[psb[pa], tmpb[ti]], writes=[cbufb[ch]])
        P.op("pool", lambda e: e.tensor_copy(out=chalo[:, l, :, :], in_=cbuf[:, :, T:T + CHL]),
             reads=cbufb, writes=[chalob])
        def conv_tap_ops(j0, j1):
            ops = []
            for j in range(j0, j1):
                for ch in range(2):
                    wcol = sp_col(l, SP_CDW + ch * 31 + j)
                    src = cbuf[:, ch, 2 + j:2 + j + T]
                    if j == 0:
                        ops.append(lambda ch=ch, wcol=wcol, src=src: P.op("dve", lambda e: e.tensor_scalar(
                            out=cacc[:, ch, :], in0=src, scalar1=wcol, scalar2=sp_col(l, SP_CDB + ch),
                            op0=ALU.mult, op1=ALU.add), reads=[cbufb[ch], spt_b], writes=[caccb[ch]]))
                    else:
                        ops.append(lambda ch=ch, wcol=wcol, src=src: P.op("dve", lambda e: e.scalar_tensor_tensor(
                            out=cacc[:, ch, :], in0=src, scalar=wcol, in1=cacc[:, ch, :], op0=ALU.mult, op1=ALU.add),
                            reads=[cbufb[ch], caccb[ch], spt_b], writes=[caccb[ch]]))
            return ops

        ps_limit[0] = 4
        nkt = 4 * (i + 1)
        tap_sched = {0: (0, 8), 1: (8, 16), 2: (16, 24), 3: (24, 31)}
        LOOK = 3
        deferred = []

        def tick():
            for d in deferred:
                d[0] -= 1
            deferred.sort(key=lambda d: d[0])
            while deferred and deferred[0][0] <= 0:
                deferred.pop(0)[1]()
                deferred.sort(key=lambda d: d[0])
        for hd in range(4):
            slot = hd % 2
            tap_ops = conv_tap_ops(*tap_sched[hd])
            taps_per_step = -(-len(tap_ops) // (8 * (i + 1)))
            bo = (4, 5)
            bd = (6, 7)
            steps = [(kt, c) for kt in range(nkt) for c in range(2)]
            st_pi = {}

            def emit_S(kt, c, hd=hd, slot=slot, st_pi=st_pi):
                j = kt - 4 * i
                q0 = 128 * j if j > 0 else 0
                pS = nps()
                if j < 0:
                    lk = kpre[slot][:, kt * 128:(kt + 1) * 128]
                    lkb = kpreb[slot]
                else:
                    lk = kcur[:, hd, j * 128:(j + 1) * 128]
                    lkb = [kcurh[hd]]
                qv = q_t[:, c, hd, q0:T]
                P.op("pe", lambda e: e.matmul(ps_t[pS][:, q0:T], lk, qv, start=True, stop=True),
                     reads=lkb + [qb[hd]], writes=[psb[pS]])
                pi = nxt("pt", NPT)
                st_pi[(kt, c)] = pi
                P.op("act", lambda e: e.activation(out=pt_t[pi][:, q0:T], in_=ps_t[pS][:, q0:T], func=AF.Exp, scale=0.125),
                     reads=[psb[pS]], writes=[ptb[pi]])
                if j >= 0:
                    P.op("dve", lambda e: e.tensor_tensor(
                        out=pt_t[pi][:, q0:q0 + 128], in0=pt_t[pi][:, q0:q0 + 128], in1=cstb[:, C_TRI:C_TRI + 128], op=ALU.mult),
                        reads=[ptb[pi], cstb_b], writes=[ptb[pi]])

            def emit_PV(kt, c, hd=hd, slot=slot, st_pi=st_pi):
                j = kt - 4 * i
                q0 = 128 * j if j > 0 else 0
                pi = st_pi[(kt, c)]
                if j < 0:
                    lv = vpre[slot][:, kt, :]
                    lvb = vpreb[slot]
                else:
                    lv = vcur[:, j, hd * 128:(hd + 1) * 128]
                    lvb = [vcurb]
                ob, db = bo[c], bd[c]
                P.op("pe", lambda e: e.matmul(ps_t[ob][:, q0:T], lv, pt_t[pi][:, q0:T], start=(kt == 0), stop=(kt == nkt - 1)),
                     reads=lvb + [ptb[pi]], writes=[psb[ob]], signal=False)
                P.op("pe", lambda e: e.matmul(ps_t[db][:, q0:T], cstb[:, C_ONES:C_ONES + 128], pt_t[pi][:, q0:T],
                                              start=(kt == 0), stop=(kt == nkt - 1)),
                     reads=[cstb_b, ptb[pi]], writes=[psb[db]], signal=(kt == nkt - 1))

            for sidx in range(len(steps) + LOOK):
                if sidx < len(steps):
                    emit_S(*steps[sidx])
                if sidx - LOOK >= 0:
                    emit_PV(*steps[sidx - LOOK])
                tick()
                for _ in range(min(len(tap_ops), taps_per_step)):
                    tap_ops.pop(0)()
            while tap_ops:
                tap_ops.pop(0)()
            ta, tb, tc, td = (nxt("tmp", NTMP) for _ in range(4))
            for tx, c in ((ta, 0), (tb, 1)):
                P.op("act", lambda e, tx=tx, c=c: e.activation(out=tmp_t[tx][:], in_=ps_t[bd[c]][:], func=AF.Ln),
                     reads=[psb[bd[c]]], writes=[tmpb[tx]])
            for tx, c in ((tc, 0), (td, 1)):
                P.op("dve", lambda e, tx=tx, c=c: e.tensor_copy(out=tmp_t[tx][:], in_=ps_t[bo[c]][:]),
                     reads=[psb[bo[c]]], writes=[tmpb[tx]])
            if hd + 2 < 4:
                kv_prefix_load(hd + 2)
            si = nxt("sq", NSQ)

            def finalize_a2(ta=ta, tb=tb, tc=tc, td=td, si=si):
                for tx in (ta, tb):
                    P.op("act", lambda e, tx=tx: e.activation(out=tmp_t[tx][:], in_=tmp_t[tx][:], func=AF.Exp, scale=-1.0),
                         reads=[tmpb[tx]], writes=[tmpb[tx]])
                for to, tr in ((tc, ta), (td, tb)):
                    P.op("dve", lambda e, to=to, tr=tr: e.tensor_tensor(out=tmp_t[to][:], in0=tmp_t[to][:], in1=tmp_t[tr][:], op=ALU.mult),
                         reads=[tmpb[to], tmpb[tr]], writes=[tmpb[to]])
                P.op("dve", lambda e: e.scalar_tensor_tensor(out=tmp_t[tc][:], in0=tmp_t[td][:], scalar=neglam[:, l:l + 1],
                                                             in1=tmp_t[tc][:], op0=ALU.mult, op1=ALU.add),
                     reads=[tmpb[tc], tmpb[td], lam_b], writes=[tmpb[tc]])

                def finalize_a3():
                    P.op("act", lambda e: e.activation(out=sq_t[si][:], in_=tmp_t[tc][:], func=AF.Square,
                                                       scale=1.0 / (1.0 - lam_init)),
                         reads=[tmpb[tc]], writes=[sqb[si]])
                deferred.append([3, finalize_a3])
            deferred.append([3, finalize_a2])

            def finalize_b(hd=hd, tc=tc, si=si):
                p2 = nps()
                P.op("pe", lambda e: e.matmul(ps_t[p2][:], onesd[:, 2, :], sq_t[si][:], start=True, stop=True),
                     reads=[sqb[si], cstb_b], writes=[psb[p2]])
                ti = rstd_from_ps(p2, EPS / (1.0 - lam_init) ** 2)
                P.op("dve", lambda e: e.scalar_tensor_tensor(
                    out=oa_t[:, hd, :], in0=tmp_t[tc][:], scalar=sp_col(l, SP_SUBLN), in1=tmp_t[ti][:], op0=ALU.mult, op1=ALU.mult),
                    reads=[tmpb[tc], tmpb[ti], spt_b], writes=[oab[hd]])
            deferred.append([10, finalize_b])
        while deferred:
            deferred.sort(key=lambda d: d[0])
            deferred.pop(0)[1]()
        ps_limit[0] = 8
        for ch in range(2):
            pz = nps()
            P.op("pe", lambda e, pz=pz, ch=ch: e.matmul(ps_t[pz][:], pwbd[:, l, ch, :], pp_t[:, ch, :], start=True, stop=True),
                 reads=[pwbdb, ppb[ch]], writes=[psb[pz]])
            P.op("act", lambda e, pz=pz, ch=ch: e.activation(out=ypool[:, ch, :], in_=ps_t[pz][:], func=AF.Identity,
                                                             scale=sp_col(l, SP_PSCALE + ch)),
                 reads=[psb[pz], spt_b], writes=[ypoolb[ch]])
        pm = nps()
        for ch in range(2):
            si = nxt("sq", NSQ)
            P.op("act", lambda e, ch=ch, si=si: e.activation(out=sq_t[si][:], in_=cacc[:, ch, :], func=AF.Copy),
                 reads=[caccb[ch]], writes=[sqb[si]])
            P.op("pe", lambda e, ch=ch, pm=pm, si=si: e.matmul(ps_t[pm][:], onesd[:, 3, :], sq_t[si][:], start=(ch == 0), stop=(ch == 1)),
                 reads=[sqb[si], cstb_b], writes=[psb[pm]], signal=True)
        pvv = nps()
        for ch in range(2):
            P.op("dve", lambda e, ch=ch, pm=pm: e.tensor_tensor(out=cacc[:, ch, :], in0=cacc[:, ch, :], in1=ps_t[pm][:], op=ALU.subtract),
                 reads=[caccb[ch], psb[pm]], writes=[caccb[ch]])
            si = nxt("sq", NSQ)
            P.op("act", lambda e, ch=ch, si=si: e.activation(out=sq_t[si][:], in_=cacc[:, ch, :], func=AF.Square),
                 reads=[caccb[ch]], writes=[sqb[si]])
            P.op("pe", lambda e, ch=ch, si=si, pvv=pvv: e.matmul(ps_t[pvv][:], onesd[:, 3, :], sq_t[si][:], start=(ch == 0), stop=(ch == 1)),
                 reads=[sqb[si], cstb_b], writes=[psb[pvv]], signal=True)
        ti = rstd_from_ps(pvv, EPS)
        for ch in range(2):
            P.op("dve", lambda e, ch=ch, ti=ti: e.tensor_tensor(out=cacc[:, ch, :], in0=cacc[:, ch, :], in1=tmp_t[ti][:], op=ALU.mult),
                 reads=[caccb[ch], tmpb[ti]], writes=[caccb[ch]])
            P.op("act", lambda e, ch=ch: e.activation(out=yconv[:, ch, :], in_=cacc[:, ch, :], func=AF.Silu,
                                                      scale=sp_col(l, SP_CLNG + ch), bias=sp_col(l, SP_CLNB + ch)),
                 reads=[caccb[ch], spt_b], writes=[yconvb[ch]])
        for m in range(8):
            wt, wb = wload(s_mrg[l, m], v3(8, 512), ("mrg", l))
            tg = []
            ysrc = ((0, 4, oa_t, oab), (4, 2, ypool, ypoolb), (6, 2, yconv, yconvb))
            for g in range(3):
                pg = nps()
                proj8(pg, wt, wb, 128 + g * 128)
                py = nps()
                c0, ncg, ysb, ybufs = ysrc[g]
                for c in range(ncg):
                    P.op("pe", lambda e, c=c, c0=c0, ncg=ncg, ysb=ysb, py=py, wt=wt: e.matmul(
                        ps_t[py][:], wt[:, c0 + c, 0:128], ysb[:, c, :], start=(c == 0), stop=(c == ncg - 1)),
                        reads=[wb, ybufs[c]], writes=[psb[py]], signal=(c == ncg - 1))
                ti = nxt("tmp", NTMP)
                tg.append(ti)
                P.op("act", lambda e, ti=ti, pg=pg, g=g: e.activation(out=tmp_t[ti][:], in_=ps_t[pg][:], func=AF.Sigmoid,
                                                                     bias=sp_col(l, SP_BGATE + g * 8 + m)),
                     reads=[psb[pg], spt_b], writes=[tmpb[ti]])
                P.op("dve", lambda e, ti=ti, py=py: e.tensor_tensor(out=tmp_t[ti][:], in0=ps_t[py][:], in1=tmp_t[ti][:], op=ALU.mult),
                     reads=[psb[py], tmpb[ti]], writes=[tmpb[ti]])
            P.op("dve", lambda e, a=tg[0], b=tg[1]: e.tensor_tensor(out=tmp_t[a][:], in0=tmp_t[a][:], in1=tmp_t[b][:], op=ALU.add),
                 reads=[tmpb[tg[0]], tmpb[tg[1]]], writes=[tmpb[tg[0]]])
            P.op("dve", lambda e, a=tg[0], b=tg[2], m=m: e.tensor_tensor(out=mrg_t[:, m, :], in0=tmp_t[a][:], in1=tmp_t[b][:], op=ALU.add),
                 reads=[tmpb[tg[0]], tmpb[tg[2]]], writes=[mrgb[m]])
        P.op("act", lambda e: e.activation(out=junk[:, 1:2], in_=epst[:, 0:1], func=AF.Ln), reads=[cstb_b], writes=[junkb])
        for s in range(2):
            wt, wb = wload(s_wout[l, s], v3(8, 512), ("mrg", l))
            for mm in range(4):
                mo = 4 * s + mm
                po = nps()
                for c in range(NCH):
                    P.op("pe", lambda e, c=c, po=po, mm=mm, wt=wt: e.matmul(
                        ps_t[po][:], wt[:, c, mm * 128:(mm + 1) * 128], mrg_t[:, c, :], start=(c == 0), stop=(c == NCH - 1)),
                        reads=[wb, mrgb[c]], writes=[psb[po]], signal=(c == NCH - 1))
                P.op("dve", lambda e, mo=mo, po=po: e.tensor_tensor(out=x_t[:, mo, :], in0=ps_t[po][:], in1=x_t[:, mo, :], op=ALU.add),
                     reads=[psb[po], xb[mo]], writes=[xb[mo]])

    def xattn(i, l):
        rmsnorm_h(l, SP_XA)
        wt, wb = wload(s_xaq[l], v3(8, 256), ("xa", l))
        pzx = [nps(), nps()]
        held.update(pzx)
        for c in range(NCH):
            for ch in range(2):
                P.op("pe", lambda e, c=c, ch=ch: e.matmul(ps_t[pzx[ch]][:], wt[:, c, ch * 128:(ch + 1) * 128], h_t[:, c, :],
                                                          start=(c == 0), stop=(c == NCH - 1)),
                     reads=[wb, hb[c]], writes=[psb[pzx[ch]]], signal=(c == NCH - 1))
        for ch in range(2):
            group_norm64(l, pzx[ch], [qx_t[0:64, 2 * ch, :], qx_t[64:128, 2 * ch + 1, :]], SP_XAQ, [qxb[2 * ch], qxb[2 * ch + 1]])
            held.discard(pzx[ch])
        wo, wob = wload(s_xao[l], v3(2, 1024), ("xa", l))
        ps_limit[0] = 4
        xsteps = [(hh, mt) for hh in range(4) for mt in range(2)]
        xpi = {}

        def x_S(hh, mt):
            ch = hh // 2
            pS = nps()
            P.op("pe", lambda e: e.matmul(ps_t[pS][:], kmem[:, l, ch, mt * 128:(mt + 1) * 128], qx_t[:, hh, :], start=True, stop=True),
                 reads=[kmemb, qxb[hh]], writes=[psb[pS]])
            pi = nxt("pt", NPT)
            xpi[(hh, mt)] = pi
            P.op("act", lambda e: e.activation(out=pt_t[pi][:], in_=ps_t[pS][:], func=AF.Exp, scale=0.125),
                 reads=[psb[pS]], writes=[ptb[pi]])

        def x_PV(hh, mt):
            ch, half = hh // 2, hh % 2
            hs = slice(half * 64, (half + 1) * 64)
            bo, bd = (4, 5) if hh % 2 == 0 else (6, 7)
            pi = xpi[(hh, mt)]
            P.op("pe", lambda e: e.matmul(ps_t[bo][:], vmem[:, l, mt, ch * 128:(ch + 1) * 128], pt_t[pi][:], start=(mt == 0), stop=(mt == 1)),
                 reads=[vmemb, ptb[pi]], writes=[psb[bo]], signal=False)
            P.op("pe", lambda e: e.matmul(ps_t[bd][:], cstb[:, C_ONES:C_ONES + 128], pt_t[pi][:], start=(mt == 0), stop=(mt == 1)),
                 reads=[cstb_b, ptb[pi]], writes=[psb[bd]], signal=(mt == 1))
            if mt == 1:
                ta = recip_from_ps(bd)
                P.op("dve", lambda e: e.tensor_tensor(out=ox_t[hs, ch, :], in0=ps_t[bo][hs, :], in1=tmp_t[ta][hs, :], op=ALU.mult),
                     reads=[psb[bo], tmpb[ta]], writes=[oxb[hh]])

        XL = 3
        for sidx in range(len(xsteps) + XL):
            if sidx < len(xsteps):
                x_S(*xsteps[sidx])
            if sidx - XL >= 0:
                x_PV(*xsteps[sidx - XL])
        ps_limit[0] = 8
        for mo in range(8):
            po = nps()
            for ch in range(2):
                P.op("pe", lambda e, ch=ch, po=po, mo=mo: e.matmul(
                    ps_t[po][:], wo[:, ch, mo * 128:(mo + 1) * 128], ox_t[:, ch, :], start=(ch == 0), stop=(ch == 1)),
                    reads=[wob, oxb[2 * ch], oxb[2 * ch + 1]], writes=[psb[po]], signal=(ch == 1))
            P.op("dve", lambda e, mo=mo, po=po: e.tensor_tensor(out=x_t[:, mo, :], in0=ps_t[po][:], in1=x_t[:, mo, :], op=ALU.add),
                 reads=[psb[po], xb[mo]], writes=[xb[mo]])

    def mem_prologue(l):
        n = NMEM
        rmsnorm_h(l, SP_MEMN, n=n)
        wt, wb = wload(s_xakv[l], v3(8, 512), ("xa", l))
        for ch in range(2):
            pz = nps()
            proj8(pz, wt, wb, ch * 128, n=n)
            group_norm64(l, pz, kmem[:, l, ch, :], SP_XAK, [kmemb], n=n)
        for mt in range(2):
            pv = nps()
            for c in range(NCH):
                P.op("pe", lambda e, c=c, pv=pv, mt=mt, wt=wt: e.matmul(
                    ps_t[pv][:, 0:256], h_t[:, c, mt * 128:(mt + 1) * 128], wt[:, c, 256:512], start=(c == 0), stop=(c == NCH - 1)),
                    reads=[wb, hb[c]], writes=[psb[pv]], signal=(c == NCH - 1))
            P.op("act", lambda e, pv=pv, mt=mt: e.activation(out=vmem[:, l, mt, :], in_=ps_t[pv][:, 0:256], func=AF.Copy),
                 reads=[psb[pv]], writes=[vmemb])

    conv_sched = []
    for l in layers:
        if "ffn1" in stages:
            conv_sched.append(lambda l=l: None)
        if "mix" in stages:
            conv_sched.append(lambda l=l: convert_mix(l))
        if "ffn2" in stages:
            conv_sched.append(lambda l=l: None)
    conv_pos = [0]

    def conv_ahead(upto):
        while conv_pos[0] < min(upto, len(conv_sched)):
            conv_sched[conv_pos[0]]()
            conv_pos[0] += 1

    if "xa" in stages:
        for l in layers:
            convert_xa(l)
    conv_ahead(2)

    if "mix" in stages:
        P.op("pool", lambda e: e.memset(pwbd[:], 0.0), writes=[pwbdb])
        for l in layers:
            for g in range(4):
                gs = slice((g % 2) * 64, (g % 2 + 1) * 64)
                P.op("pool", lambda e, l=l, g=g, gs=gs: e.dma_start(out=pwbd[gs, l, g // 2, gs], in_=pool_w[l, g]),
                     writes=[pwbdb], dma=pwbdb)
            lam_init = 0.8 - 0.6 * math.exp(-0.3 * l)
            t_l, t_p, t_s = 0, 1, 2
            P.op("sp", lambda e, l=l: e.dma_start(out=tmp_t[t_l][:, 0:256], in_=lamb[l]), writes=[tmpb[t_l]], dma=tmpb[t_l])
            for k in range(2):
                P.op("dve", lambda e, k=k: e.tensor_tensor(out=tmp_t[t_p][:, k * 64:(k + 1) * 64], in0=tmp_t[t_l][:, k * 128:k * 128 + 64],
                                                           in1=tmp_t[t_l][:, k * 128 + 64:k * 128 + 128], op=ALU.mult),
                     reads=[tmpb[t_l]], writes=[tmpb[t_p]])
                P.op("dve", lambda e, k=k: e.reduce_sum(out=tmp_t[t_s][:, k:k + 1], in_=tmp_t[t_p][:, k * 64:(k + 1) * 64],
                                                        axis=mybir.AxisListType.X),
                     reads=[tmpb[t_p]], writes=[tmpb[t_s]])
            P.op("act", lambda e: e.activation(out=tmp_t[t_s][:, 2:4], in_=tmp_t[t_s][:, 0:2], func=AF.Exp),
                 reads=[tmpb[t_s]], writes=[tmpb[t_s]])
            P.op("dve", lambda e, l=l, lam_init=lam_init: e.scalar_tensor_tensor(
                out=neglam[:, l:l + 1], in0=tmp_t[t_s][:, 3:4], scalar=-lam_init, in1=tmp_t[t_s][:, 2:3],
                op0=ALU.add, op1=ALU.subtract), reads=[tmpb[t_s]], writes=[lam_b])
    if "xa" in stages:
        P.op("sp", lambda e: e.dma_start(out=x_t[:, :, 0:NMEM], in_=memT.rearrange("(c p) t -> p c t", p=128)),
             writes=xb + [x_dma], dma=x_dma)
        for l in layers:
            mem_prologue(l)

    stage_no = [0]
    for i in range(ntiles):
        t0 = i * T
        P.op("sp", lambda e, t0=t0: e.dma_start(out=x_t[:], in_=xT[:, t0:t0 + T].rearrange("(c p) t -> p c t", p=128)),
             writes=xb + [x_dma], dma=x_dma)
        for l in layers:
            if "ffn1" in stages:
                stage_no[0] += 1
                conv_ahead(stage_no[0] + 2)
                ffn(0, l, direct=(i == 0))
            if "mix" in stages:
                stage_no[0] += 1
                conv_ahead(stage_no[0] + 2)
                mixer(i, l)
            if "xa" in stages:
                xattn(i, l)
            if "ffn2" in stages:
                stage_no[0] += 1
                conv_ahead(stage_no[0] + 2)
                ffn(1, l, direct=(i == 0))
        P.op("sp", lambda e, t0=t0: e.dma_start(out=yT[:, t0:t0 + T].rearrange("(c p) t -> p c t", p=128), in_=x_t[:]),
             reads=xb, writes=[x_dma], dma=x_dma)
    P.op("sp", None, reads=[x_dma], writes=[x_dma], signal=False)
    P.engs["sp"].pending = []

    P.check()
    P.emit(nc, st)
    st.close()
    return nc, P


def _col(v, n):
    return np.ascontiguousarray(np.asarray(v, np.float32).reshape(n, 128).T)


def pack_spar(inp):
    sp = np.zeros((DEPTH, 128, NSP), np.float32)
    for l in range(DEPTH):
        sp[l, :, SP_FFN1:SP_FFN1 + 8] = _col(inp["ffn1_norm"][l], 8)
        sp[l, :, SP_MIX:SP_MIX + 8] = _col(inp["mix_norm"][l], 8)
        sp[l, :, SP_XA:SP_XA + 8] = _col(inp["xa_norm"][l], 8)
        sp[l, :, SP_MEMN:SP_MEMN + 8] = _col(inp["xa_mem_norm"][l], 8)
        sp[l, :, SP_FFN2:SP_FFN2 + 8] = _col(inp["ffn2_norm"][l], 8)
        sp[l, :, SP_BGATE:SP_BGATE + 24] = _col(inp["b_gate"][l], 24)
        sp[l, :, SP_DAQ] = np.tile(np.asarray(inp["da_q_norm"][l], np.float32), 2)
        sp[l, :, SP_DAK] = np.tile(np.asarray(inp["da_k_norm"][l], np.float32), 2)
        sp[l, :, SP_SUBLN] = np.asarray(inp["da_subln"][l], np.float32)
        sp[l, :, SP_PSCALE:SP_PSCALE + 2] = _col(inp["pool_scale"][l], 2)
        sp[l, :, SP_CDB:SP_CDB + 2] = _col(inp["conv_db"][l], 2)
        sp[l, :, SP_CLNG:SP_CLNG + 2] = _col(inp["conv_ln_g"][l], 2)
        sp[l, :, SP_CLNB:SP_CLNB + 2] = _col(inp["conv_ln_b"][l], 2)
        sp[l, :, SP_XAQ] = np.tile(np.asarray(inp["xa_q_norm"][l], np.float32), 2)
        sp[l, :, SP_XAK] = np.tile(np.asarray(inp["xa_k_norm"][l], np.float32), 2)
        cdw = np.asarray(inp["conv_dw"][l], np.float32)
        sp[l, :, SP_CDW:SP_CDW + 62] = cdw.T.reshape(2, 128, 31).transpose(1, 0, 2).reshape(128, 62)
    return sp


def make_consts():
    c = np.zeros((128, NCONST), np.float32)
    c[:, C_ONES:C_ONES + 128] = 1.0
    for g in range(2):
        c[g * 64:(g + 1) * 64, C_BLK64 + g * 64:C_BLK64 + (g + 1) * 64] = 1.0
    k = np.arange(128)[:, None]
    q = np.arange(128)[None, :]
    c[:, C_TRI:C_TRI + 128] = (q >= k).astype(np.float32)
    wins = (2, 4, 8, 16)
    for ch in range(2):
        for half in range(2):
            w = wins[2 * ch + half]
            c[half * 64:(half + 1) * 64, C_RW + ch] = 1.0 / w
            t = np.arange(16)
            c[half * 64:(half + 1) * 64, C_RC0 + ch * 16:C_RC0 + (ch + 1) * 16] = 1.0 / np.minimum(t + 1, w)
    return c


_CACHE = {}


def kernel(**inputs):
    inp = {k: np.asarray(v) for k, v in inputs.items()}
    if "nc" not in _CACHE:
        _CACHE["nc"] = build()[0]
    nc = _CACHE["nc"]
    x = inp["x"].astype(np.float32, copy=False)
    mem = inp["mem"].astype(np.float32, copy=False)
    spar = pack_spar(inp)
    consts = make_consts()
    lamb = np.ascontiguousarray(np.broadcast_to(
        np.asarray(inp["da_lambda"], np.float32).reshape(DEPTH, 1, 256), (DEPTH, 128, 256)))
    shared = {
        "spar": spar, "consts": consts, "lamb": lamb,
        "ffn1_w_gu": inp["ffn1_w_gu"], "ffn2_w_gu": inp["ffn2_w_gu"],
        "ffn1_w_down": inp["ffn1_w_down"], "ffn2_w_down": inp["ffn2_w_down"],
        "w_in": inp["w_in"], "w_proj_attn": inp["w_proj_attn"], "w_proj_pool": inp["w_proj_pool"],
        "w_proj_conv": inp["w_proj_conv"], "w_out": inp["w_out"], "pool_w": inp["pool_w"],
        "xa_w_q": inp["xa_w_q"], "xa_w_kv": inp["xa_w_kv"], "xa_w_o": inp["xa_w_o"],
    }
    in_maps = []
    for b in range(NCORES):
        m = dict(shared)
        m["xT"] = np.ascontiguousarray(x[b].T)
        m["memT"] = np.ascontiguousarray(mem[b].T)
        in_maps.append(m)
    res = run_bass_kernel_spmd(nc, in_maps, core_ids=list(range(NCORES)))
    out = np.stack([np.ascontiguousarray(r["yT"].T) for r in res.results], axis=0)
    return out.astype(np.float32, copy=False)
```

```python
from contextlib import ExitStack
import math
import numpy as np
import concourse.bass as bass
import concourse.mybir as mybir
from concourse.bass_utils import run_bass_kernel_spmd

F32 = mybir.dt.float32
BF16 = mybir.dt.bfloat16
AF = mybir.ActivationFunctionType
ALU = mybir.AluOpType

D = 1024
S = 4096
DEPTH = 4
NMEM = 256
DFF = 2816
NCH = 8
FCH = 22
T = 512
NT = S // T
EPS = 1e-6
INC = 5376
NCORES = 8


class Buf:
    __slots__ = ("name", "w", "r", "dsem", "dcount")

    def __init__(self, name, dsem=None):
        self.name = name
        self.w = {}
        self.r = {}
        self.dsem = dsem
        self.dcount = 0


class Op:
    __slots__ = ("eng", "fn", "waits", "sig", "tok", "idx")


class Eng:
    def __init__(self, name):
        self.name = name
        self.ops = []
        self.count = 0
        self.pending = []
        self.waited = {}
        self.semkey = "e_" + name


class Prog:
    def __init__(self):
        self.engs = {n: Eng(n) for n in ("pe", "act", "dve", "pool", "sp")}
        self.semkeys = [e.semkey for e in self.engs.values()]
        self.nbuf = 0

    def buf(self, name, dma=False):
        self.nbuf += 1
        dsem = None
        if dma:
            dsem = "d_" + name
            assert dsem not in self.semkeys
            self.semkeys.append(dsem)
        return Buf(name, dsem)

    def op(self, eng, fn, reads=(), writes=(), signal=True, dma=None):
        E = self.engs[eng]
        o = Op()
        o.eng = E
        o.fn = fn
        o.tok = None
        o.sig = None
        deps = []
        for b in reads:
            deps.extend(b.w.values())
        for b in writes:
            deps.extend(b.w.values())
            deps.extend(b.r.values())
        waits = []
        for d in deps:
            if d.eng is E and eng == "pe" and d.tok is None:
                continue
            if d.eng is E and eng == "pe" and d.tok[0] == E.semkey:
                continue
            assert d.tok is not None, "dependency on unsignalled op"
            k, v = d.tok
            if E.waited.get(k, 0) < v:
                E.waited[k] = v
        o.waits = waits
        o.idx = len(E.ops)
        E.ops.append(o)
        if dma is not None:
            dma.dcount += 16
            o.tok = (dma.dsem, dma.dcount)
            o.sig = (dma.dsem, 16)
        elif signal:
            E.count += 1
            o.tok = (E.semkey, E.count)
            o.sig = (E.semkey, 1)
            for p in E.pending:
                p.tok = o.tok
            E.pending = []
        else:
            E.pending.append(o)
        for b in reads:
            b.r[o.tok[0] if o.tok else ("pend", E.name)] = o
        for b in writes:
            b.w = {o.tok[0] if o.tok else ("pend", E.name): o}
            b.r = {}
        return o


class Prog2(Prog):
    def op(self, eng, fn, reads=(), writes=(), signal=True, dma=None):
        E = self.engs[eng]
        before = dict(E.waited)
        o = Prog.op(self, eng, fn, reads, writes, signal, dma)
        o.waits = [(k, v) for k, v in E.waited.items() if before.get(k, 0) < v]
        return o

    def check(self):
        sem = {k: 0 for k in self.semkeys}
        pos = {n: 0 for n in self.engs}
        progress = True
        while progress:
            progress = False
            for n, E in self.engs.items():
                while pos[n] < len(E.ops):
                    o = E.ops[pos[n]]
                    if all(sem[k] >= v for k, v in o.waits):
                        if o.sig:
                            sem[o.sig[0]] += o.sig[1]
                        pos[n] += 1
                        progress = True
                    else:
                        break
        for n, E in self.engs.items():
            if pos[n] < len(E.ops):
                o = E.ops[pos[n]]
                raise RuntimeError(
                    f"deadlock: engine {n} stuck at op {pos[n]}/{len(E.ops)} waits={o.waits} "
                    f"sems={ {k: sem[k] for k, _ in o.waits} }")

    def emit(self, nc, stack):
        for E in self.engs.values():
            assert not E.pending or E.name == "pe", E.name
        sems = {k: stack.enter_context(nc.semaphore(k)) for k in self.semkeys}
        engs = self.engs

        def run(name):
            def f(e):
                fold = name in ("pe", "act", "dve")
                for o in engs[name].ops:
                    waits = list(o.waits)
                    last = waits.pop() if (fold and waits and o.fn is not None) else None
                    for k, v in waits:
                        e.wait_ge(sems[k], v)
                    if o.fn is None:
                        continue
                    ins = o.fn(e)
                    if last is not None:
                        ins._wait_ge(sems[last[0]], last[1])
                    if o.sig:
                        ins.then_inc(sems[o.sig[0]], o.sig[1])
            return f

        with nc.Block() as block:
            block.tensor(run("pe"))
            block.scalar(run("act"))
            block.vector(run("dve"))
            block.gpsimd(run("pool"))
            block.sync(run("sp"))


SP_FFN1 = 0
SP_MIX = 8
SP_XA = 16
SP_MEMN = 24
SP_FFN2 = 32
SP_BGATE = 40
SP_DAQ = 64
SP_DAK = 65
SP_SUBLN = 66
SP_PSCALE = 67
SP_CDB = 69
SP_CLNG = 71
SP_CLNB = 73
SP_XAQ = 75
SP_XAK = 76
SP_CDW = 77
NSP = 139

C_ONES = 0
C_BLK64 = 128
C_TRI = 256
C_RW = 384
C_RC0 = 386
NCONST = 418

WSLOT_ELEMS = 4096
NWSLOT = 3
PH = 16
CHL = 32


def build(layers=(0, 1, 2, 3), ntiles=NT, stages=("ffn1", "mix", "xa", "ffn2")):
    nc = bass.Bass("TRN2", target_bir_lowering=False)
    L = DEPTH

    def din(name, shape, dt=F32):
        return nc.dram_tensor(name, list(shape), dt, kind="ExternalInput").ap()

    def dint(name, shape, dt=BF16):
        return nc.dram_tensor(name, list(shape), dt, kind="Internal").ap()

    xT = din("xT", [D, S])
    memT = din("memT", [D, NMEM])
    spar = din("spar", [L, 128, NSP])
    lamb = din("lamb", [L, 128, 256])
    consts = din("consts", [128, NCONST])
    w_gu = [din("ffn1_w_gu", [L, D, 2 * DFF]), din("ffn2_w_gu", [L, D, 2 * DFF])]
    w_dn = [din("ffn1_w_down", [L, DFF, D]), din("ffn2_w_down", [L, DFF, D])]
    w_in = din("w_in", [L, D, INC])
    w_pa = din("w_proj_attn", [L, 512, D])
    w_pp = din("w_proj_pool", [L, 256, D])
    w_pc = din("w_proj_conv", [L, 256, D])
    w_out = din("w_out", [L, D, D])
    pool_w = din("pool_w", [L, 4, 64, 64])
    xa_wq = din("xa_w_q", [L, D, 256])
    xa_wkv = din("xa_w_kv", [L, D, 512])
    xa_wo = din("xa_w_o", [L, 256, D])
    yT = nc.dram_tensor("yT", [D, S], F32, kind="ExternalOutput").ap()

    s_gu = [dint(f"s_gu{f}", [L, 11, 128, 2, 8, 256]) for f in range(2)]
    s_dn = [dint(f"s_dn{f}", [L, 8, 128, 22, 128]) for f in range(2)]
    s_qkvc = dint("s_qkvc", [L, 4, 128, 8, 512])
    s_pool = dint("s_pool", [L, 128, 8, 256])
    s_mrg = dint("s_mrg", [L, 8, 128, 8, 512])
    s_wout = dint("s_wout", [L, 2, 128, 8, 512])
    s_xaq = dint("s_xaq", [L, 128, 8, 256])
    s_xakv = dint("s_xakv", [L, 128, 8, 512])
    s_xao = dint("s_xao", [L, 128, 2, 1024])
    kcache = dint("kcache", [L, 4, 128, S])
    vcache = dint("vcache", [L, 4, 128, S // 128, 128])

    P = Prog2()
    st = ExitStack()

    def sb(name, shape, dt):
        return st.enter_context(nc.sbuf_tensor(name, list(shape), dt))

    x_t = sb("x_t", [128, NCH, T], F32)
    h_t = sb("h_t", [128, NCH, T], BF16)
    a_raw = sb("a_t", [128, FCH * T], BF16)
    a_t = a_raw[:, :].rearrange("p (c t) -> p c t", c=FCH, t=T)
    KPRE = 3584
    kpre = [a_raw[:, 0:KPRE], a_raw[:, KPRE:2 * KPRE]]
    vpre0 = a_raw[:, 2 * KPRE:3 * KPRE].rearrange("p (k e) -> p k e", k=28, e=128)
    vpre1_raw = sb("vpre1", [128, 28, 128], BF16)
    vpre = [vpre0, vpre1_raw[:, :, :]]
    wsl = [sb(f"wsl{i}", [128, WSLOT_ELEMS], BF16) for i in range(NWSLOT)]
    NSTG = 3
    stg_t = [sb(f"stg{i}", [128, 2048], F32) for i in range(NSTG)]
    cst = sb("cst", [128, NCONST], F32)
    cstb = sb("cstb", [128, 384], BF16)
    onesd = sb("onesd", [128, 4, 128], BF16)
    ones256f = sb("ones256f", [128, 128], F32)
    epst = sb("epst", [128, 8], F32)
    junk = sb("junk", [128, 8], F32)
    spt = sb("spt", [128, L, NSP], F32)
    neglam = sb("neglam", [128, L], F32)
    NSQ = 3
    sq_t = [sb(f"sq{i}", [128, T], BF16) for i in range(NSQ)]
    NTMP = 8
    tmp_t = [sb(f"tmp{i}", [128, T], F32) for i in range(NTMP)]
    ps_t = [st.enter_context(nc.psum_tensor(f"ps{i}", [128, T], F32)) for i in range(8)]
    q_t = sb("q_t", [128, 2, 4, T], BF16)
    kcur = sb("kcur", [128, 4, T], BF16)
    vcur = sb("vcur", [128, 4, T], BF16)
    NPT = 7
    pt_t = [sb(f"pt{i}", [128, T], BF16) for i in range(NPT)]
    pbuf = sb("pbuf", [128, 2, PH + T], F32)
    wA = sb("wA", [128, 2, PH + T], F32)
    wB = sb("wB", [128, 2, PH + T], F32)
    pp_t = sb("pp_t", [128, 2, T], BF16)
    ypool = sb("ypool", [128, 2, T], BF16)
    cbuf = sb("cbuf", [128, 2, CHL + T], F32)
    cacc = sb("cacc", [128, 2, T], F32)
    yconv = sb("yconv", [128, 2, T], BF16)
    oa_t = sb("oa_t", [128, 4, T], BF16)
    mrg_t = sb("mrg_t", [128, NCH, T], BF16)
    qx_t = sb("qx_t", [128, 4, T], BF16)
    ox_t = sb("ox_t", [128, 2, T], BF16)
    kmem = sb("kmem", [128, L, 2, NMEM], BF16)
    vmem = sb("vmem", [128, L, 2, NMEM], BF16)
    pwbd = sb("pwbd", [128, L, 2, 128], BF16)
    phalo = sb("phalo", [128, L, 2, PH], F32)
    chalo = sb("chalo", [128, L, 2, CHL], F32)

    B = P.buf
    xb = [B(f"x{c}") for c in range(NCH)]
    x_dma = B("xdma", dma=True)
    hb = [B(f"h{c}") for c in range(NCH)]
    ab = [B(f"a{c}") for c in range(FCH)]
    kpre_d = [B("kpre0", dma=True), B("kpre1", dma=True)]
    vpre_d = [B("vpre0", dma=True), B("vpre1", dma=True)]
    kpreb = [[kpre_d[0]] + ab[0:7], [kpre_d[1]] + ab[7:14]]
    vpreb = [[vpre_d[0]] + ab[14:21], [vpre_d[1]]]
    wslb = [B(f"wsl{i}", dma=True) for i in range(NWSLOT)]
    stgb = [B(f"stg{i}", dma=True) for i in range(NSTG)]
    cst_b = B("cst", dma=True)
    cstb_b = B("cstb")
    spt_b = B("spt", dma=True)
    lam_b = B("lam")
    junkb = B("junk")
    sqb = [B(f"sq{i}") for i in range(NSQ)]
    tmpb = [B(f"tmp{i}", dma=True) for i in range(NTMP)]
    psb = [B(f"ps{i}") for i in range(8)]
    qb = [B(f"q{h}") for h in range(4)]
    kcurb = B("kcur", dma=True)
    kcurh = [B(f"kcur{h}") for h in range(4)]
    vcurb = B("vcur", dma=True)
    ptb = [B(f"pt{i}") for i in range(NPT)]
    pbufb = B("pbuf")
    wAb = [B("wA0"), B("wA1")]
    wBb = [B("wB0"), B("wB1")]
    ppb = [B("pp0"), B("pp1")]
    ypoolb = [B("ypool0"), B("ypool1")]
    cbufb = [B("cbuf0"), B("cbuf1")]
    caccb = [B("cacc0"), B("cacc1")]
    ctmpb = B("ctmp")
    yconvb = [B("yconv0"), B("yconv1")]
    oab = [B(f"oa{h}") for h in range(4)]
    daccb = [B(f"dacc{k}") for k in range(4)]
    mrgb = [B(f"mrg{c}") for c in range(NCH)]
    qxb = [B(f"qx{h}") for h in range(4)]
    oxb = [B(f"ox{h}") for h in range(4)]
    kmemb = B("kmem")
    vmemb = B("vmem")
    pwbdb = B("pwbd", dma=True)
    phalob = B("phalo")
    chalob = B("chalo")
    kcacheb = [B(f"kcache{l}") for l in range(L)]
    vcacheb = [B(f"vcache{l}") for l in range(L)]
    conv_b = {}

    ctr = {"ws": 0, "sq": 0, "tmp": 0, "ps": 0, "pt": 0, "pss": 0, "stg": 0, "cast": 0}
    ps_limit = [8]

    def nxt(kind, n):
        i = ctr[kind] % n
        ctr[kind] += 1
        return i

    held = set()

    def nps():
        while True:
            b = nxt("ps", ps_limit[0])
            if b not in held:
                return b

    def sp_col(l, col, n=1):
        return spt[:, l, col:col + n]

    P.op("sp", lambda e: e.dma_start(out=cst[:], in_=consts), writes=[cst_b], dma=cst_b)
    P.op("sp", lambda e: e.dma_start(out=spt[:], in_=spar.rearrange("l p n -> p l n")),
         writes=[spt_b], dma=spt_b)
    P.op("dve", lambda e: e.tensor_copy(out=cstb[:], in_=cst[:, 0:384]), reads=[cst_b], writes=[cstb_b])
    for k, (c0, val) in enumerate(((C_ONES, 1.0 / D), (C_BLK64, 1.0 / 64), (C_ONES, 1.0 / 128), (C_ONES, 1.0 / 256))):
        P.op("dve", lambda e, k=k, c0=c0, val=val: e.tensor_scalar(out=onesd[:, k, :], in0=cst[:, c0:c0 + 128],
                                                                   scalar1=val, scalar2=None, op0=ALU.mult),
             reads=[cst_b], writes=[cstb_b])
    P.op("dve", lambda e: e.tensor_scalar(out=ones256f[:], in0=cst[:, C_ONES:C_ONES + 128], scalar1=1.0 / 256,
                                          scalar2=None, op0=ALU.mult), reads=[cst_b], writes=[cstb_b])
    eps_vals = [EPS] + [EPS / (1.0 - (0.8 - 0.6 * math.exp(-0.3 * l))) ** 2 for l in range(L)]
    epsb = {}
    for k, v in enumerate(eps_vals):
        epsb[v] = epst[:, k:k + 1]
        P.op("pool", lambda e, k=k, v=v: e.memset(epst[:, k:k + 1], v), writes=[cstb_b])

    P.op("pool", lambda e: e.memset(q_t[:], 0.0), writes=qb)
    P.op("pool", lambda e: e.memset(qx_t[:], 0.0), writes=qxb)

    def conv_op(key, out_ap, in_ap):
        b = conv_b.get(key)
        if b is None:
            b = conv_b[key] = B("cv_%s_%d" % key, dma=True)
        P.op("pool", lambda e: e.dma_start(out=out_ap, in_=in_ap), writes=[b], dma=b)

    def convert_ffn(f, l):
        for s in range(11):
            src = w_gu[f][l].rearrange("(c p) (g s f) -> s p g c f", c=8, p=128, g=2, s=11, f=256)[s]
            conv_op(("gu%d" % f, l), s_gu[f][l, s], src)
        for s in range(8):
            src = w_dn[f][l].rearrange("(c p) (s f) -> s p c f", c=22, p=128, s=8, f=128)[s]
            conv_op(("dn%d" % f, l), s_dn[f][l, s], src)

    def kc(ap):
        return ap.rearrange("(c p) f -> p c f", p=128)

    def convert_mix(l):
        for s, c0 in enumerate((0, 512, 1024, 1792)):
            conv_op(("qkvc", l), s_qkvc[l, s], kc(w_in[l][:, c0:c0 + 512]))
        conv_op(("qkvc", l), s_pool[l], kc(w_in[l][:, 1536:1792]))
        for m in range(8):
            ms = slice(m * 128, (m + 1) * 128)
            conv_op(("mrg", l), s_mrg[l, m][:, 0:4, 0:128], kc(w_pa[l][:, ms]))
            conv_op(("mrg", l), s_mrg[l, m][:, 4:6, 0:128], kc(w_pp[l][:, ms]))
            conv_op(("mrg", l), s_mrg[l, m][:, 6:8, 0:128], kc(w_pc[l][:, ms]))
            for g in range(3):
                c0 = 2304 + g * 1024 + m * 128
                conv_op(("mrg", l), s_mrg[l, m][:, :, 128 + g * 128:256 + g * 128], kc(w_in[l][:, c0:c0 + 128]))
        for s in range(2):
            conv_op(("mrg", l), s_wout[l, s], kc(w_out[l][:, s * 512:(s + 1) * 512]))

    def convert_xa(l):
        conv_op(("xa", l), s_xaq[l], kc(xa_wq[l]))
        conv_op(("xa", l), s_xakv[l], kc(xa_wkv[l]))
        conv_op(("xa", l), s_xao[l], kc(xa_wo[l]))

    def wload(src_ap, view, key, nparts=128):
        i = nxt("ws", NWSLOT)
        n = 1
        for d in src_ap.shape[1:]:
            n *= d
        assert n <= WSLOT_ELEMS, n
        dst_flat = wsl[i][0:nparts, 0:n]
        src_flat = src_ap
        if len(src_ap.shape) > 2:
            names = " ".join("d%d" % k for k in range(len(src_ap.shape) - 1))
            src_flat = src_ap.rearrange("p %s -> p (%s)" % (names, names))
        P.op("sp", lambda e: e.dma_start(out=dst_flat, in_=src_flat), reads=[conv_b[key]],
             writes=[wslb[i]], dma=wslb[i])
        return view(wsl[i]), wslb[i]

    def wload_direct(pieces, n, scratch_slab, view, key):
        i = nxt("ws", NWSLOT)
        b = conv_b.get(key)
        if b is None:
            b = conv_b[key] = B("cv_%s_%d" % key, dma=True)
        for src, off in pieces:
            a_, b_ = src.shape[1], src.shape[2]
            ne = a_ * b_
            u = nxt("stg", NSTG)
            sview = stg_t[u][:, 0:ne].rearrange("p (a b) -> p a b", a=a_, b=b_)
            P.op("sp", lambda e, sview=sview, src=src: e.dma_start(out=sview, in_=src), writes=[stgb[u]], dma=stgb[u])
            eng = "act" if nxt("cast", 2) == 0 else "dve"
            if eng == "act":
                P.op("act", lambda e, u=u, off=off, ne=ne, i=i: e.activation(out=wsl[i][:, off:off + ne], in_=stg_t[u][:, 0:ne], func=AF.Copy),
                     reads=[stgb[u]], writes=[wslb[i]])
            else:
                P.op("dve", lambda e, u=u, off=off, ne=ne, i=i: e.tensor_copy(out=wsl[i][:, off:off + ne], in_=stg_t[u][:, 0:ne]),
                     reads=[stgb[u]], writes=[wslb[i]])
        flat = scratch_slab
        if len(flat.shape) > 2:
            names = " ".join("d%d" % k for k in range(len(flat.shape) - 1))
            flat = flat.rearrange("p %s -> p (%s)" % (names, names))
        P.op("sp", lambda e, i=i, flat=flat: e.dma_start(out=flat, in_=wsl[i][:, 0:n]), reads=[wslb[i]], writes=[b], dma=b)
        return view(wsl[i]), wslb[i]

    def v3(a, b):
        return lambda t: t[:, 0:a * b].rearrange("p (c f) -> p c f", c=a, f=b)

    def rstd_from_ps(pi, eps, n=T):
        ti = nxt("tmp", NTMP)
        P.op("act", lambda e, ti=ti, pi=pi: e.activation(out=tmp_t[ti][:, 0:n], in_=ps_t[pi][:, 0:n], func=AF.Ln,
                                                        bias=epsb[eps][:, 0:1]),
             reads=[psb[pi], cstb_b], writes=[tmpb[ti]])
        P.op("act", lambda e, ti=ti: e.activation(out=tmp_t[ti][:, 0:n], in_=tmp_t[ti][:, 0:n], func=AF.Exp, scale=-0.5),
             reads=[tmpb[ti]], writes=[tmpb[ti]])
        return ti

    def recip_from_ps(pi, n=T):
        ti = nxt("tmp", NTMP)
        P.op("act", lambda e, ti=ti, pi=pi: e.activation(out=tmp_t[ti][:, 0:n], in_=ps_t[pi][:, 0:n], func=AF.Ln),
             reads=[psb[pi]], writes=[tmpb[ti]])
        P.op("act", lambda e, ti=ti: e.activation(out=tmp_t[ti][:, 0:n], in_=tmp_t[ti][:, 0:n], func=AF.Exp, scale=-1.0),
             reads=[tmpb[ti]], writes=[tmpb[ti]])
        return ti

    def rmsnorm_h(l, gcol, n=T):
        pi = nps()
        for c in range(NCH):
            si = nxt("sq", NSQ)
            P.op("act", lambda e, c=c, si=si: e.activation(out=sq_t[si][:, 0:n], in_=x_t[:, c, 0:n], func=AF.Square),
                 reads=[xb[c]], writes=[sqb[si]])
            P.op("pe", lambda e, c=c, si=si, pi=pi: e.matmul(ps_t[pi][:, 0:n], onesd[:, 0, :], sq_t[si][:, 0:n],
                                                            start=(c == 0), stop=(c == NCH - 1)),
                 reads=[sqb[si], cstb_b], writes=[psb[pi]], signal=True)
        ti = rstd_from_ps(pi, EPS, n)
        for c in range(NCH):
            P.op("dve", lambda e, c=c, ti=ti: e.scalar_tensor_tensor(out=h_t[:, c, 0:n], in0=x_t[:, c, 0:n],
                                                                     scalar=sp_col(l, gcol + c), in1=tmp_t[ti][:, 0:n],
                                                                     op0=ALU.mult, op1=ALU.mult),
                 reads=[xb[c], tmpb[ti], spt_b], writes=[hb[c]])

    def proj8(pi, wt, wb, col0, n=T, ncols=128):
        for c in range(NCH):
            P.op("pe", lambda e, c=c: e.matmul(ps_t[pi][0:ncols, 0:n], wt[:, c, col0:col0 + ncols], h_t[:, c, 0:n],
                                               start=(c == 0), stop=(c == NCH - 1)),
                 reads=[wb, hb[c]], writes=[psb[pi]], signal=(c == NCH - 1))

    def group_norm64(l, pz, out_ap, gcol, out_bufs, n=T):
        si = nxt("sq", NSQ)
        P.op("act", lambda e: e.activation(out=sq_t[si][:, 0:n], in_=ps_t[pz][:, 0:n], func=AF.Square),
             reads=[psb[pz]], writes=[sqb[si]])
        was_held = pz in held
        held.add(pz)
        p2 = nps()
        if not was_held:
            held.discard(pz)
        P.op("pe", lambda e: e.matmul(ps_t[p2][:, 0:n], onesd[:, 1, :], sq_t[si][:, 0:n], start=True, stop=True),
             reads=[sqb[si], cstb_b], writes=[psb[p2]])
        ti = rstd_from_ps(p2, EPS, n)
        if isinstance(out_ap, (list, tuple)):
            for half, oap in enumerate(out_ap):
                hs = slice(half * 64, (half + 1) * 64)
                P.op("dve", lambda e, hs=hs, oap=oap: e.scalar_tensor_tensor(
                    out=oap, in0=ps_t[pz][hs, 0:n], scalar=spt[hs, l, gcol:gcol + 1],
                    in1=tmp_t[ti][hs, 0:n], op0=ALU.mult, op1=ALU.mult),
                    reads=[psb[pz], tmpb[ti], spt_b], writes=[out_bufs[half]] if len(out_bufs) == 2 else out_bufs)
        else:
            P.op("dve", lambda e: e.scalar_tensor_tensor(out=out_ap, in0=ps_t[pz][:, 0:n], scalar=sp_col(l, gcol),
                                                         in1=tmp_t[ti][:, 0:n], op0=ALU.mult, op1=ALU.mult),
                 reads=[psb[pz], tmpb[ti], spt_b], writes=out_bufs)

    def lagged(items, lag=2):
        for k in range(len(items) + lag):
            if k < len(items):
                items[k][0]()
            if k - lag >= 0:
                items[k - lag][1]()

    def ffn(f, l, direct=False):
        rmsnorm_h(l, SP_FFN1 if f == 0 else SP_FFN2)
        for s in range(11):
            guview = lambda t: t[:, 0:4096].rearrange("p (g c f) -> p g c f", g=2, c=8, f=256)
            if direct:
                pieces = [(kc(w_gu[f][l][:, g * DFF + s * 256:g * DFF + (s + 1) * 256]), g * 2048) for g in range(2)]
                wt, wb = wload_direct(pieces, 4096, s_gu[f][l, s], guview, ("gu%d" % f, l))
            else:
                wt, wb = wload(s_gu[f][l, s], guview, ("gu%d" % f, l))
            banks = [(nps(), nps()) for _ in range(2)]
            if s == 0:
                for c in range(NCH):
                    for jj in range(2):
                        for g in range(2):
                            pi = banks[jj][g]
                            P.op("pe", lambda e, c=c, jj=jj, g=g, pi=pi, wt=wt: e.matmul(
                                ps_t[pi][:], wt[:, g, c, jj * 128:(jj + 1) * 128], h_t[:, c, :],
                                start=(c == 0), stop=(c == NCH - 1)),
                                reads=[wb, hb[c]], writes=[psb[pi]], signal=(c == NCH - 1))
            for jj in range(2):
                j = 2 * s + jj
                pg, pu = banks[jj]
                if s > 0:
                    proj8(pg, wt[:, 0], wb, jj * 128)
                    proj8(pu, wt[:, 1], wb, jj * 128)
                ti = nxt("tmp", NTMP)
                P.op("act", lambda e, ti=ti, pg=pg: e.activation(out=tmp_t[ti][:], in_=ps_t[pg][:], func=AF.Silu),
                     reads=[psb[pg]], writes=[tmpb[ti]])
                P.op("dve", lambda e, ti=ti, pu=pu, j=j: e.tensor_tensor(out=a_t[:, j, :], in0=ps_t[pu][:],
                                                                         in1=tmp_t[ti][:], op=ALU.mult),
                     reads=[psb[pu], tmpb[ti]], writes=[ab[j]])
        P.op("act", lambda e: e.activation(out=junk[:, 0:1], in_=epst[:, 0:1], func=AF.Ln), reads=[cstb_b], writes=[junkb])
        for m in range(8):
            if direct:
                srcm = kc(w_dn[f][l][:, m * 128:(m + 1) * 128])
                pieces = [(srcm[:, 11 * hh:11 * hh + 11, :], hh * 1408) for hh in range(2)]
                wt, wb = wload_direct(pieces, 2816, s_dn[f][l, m], v3(22, 128), ("dn%d" % f, l))
            else:
                wt, wb = wload(s_dn[f][l, m], v3(22, 128), ("dn%d" % f, l))
            po = nps()
            for c in range(FCH):
                P.op("pe", lambda e, c=c, po=po, wt=wt: e.matmul(
                    ps_t[po][:], wt[:, c, :], a_t[:, c, :], start=(c == 0), stop=(c == FCH - 1)),
                    reads=[wb, ab[c]], writes=[psb[po]], signal=(c == FCH - 1))
            P.op("dve", lambda e, m=m, po=po: e.scalar_tensor_tensor(
                out=x_t[:, m, :], in0=ps_t[po][:], scalar=0.5, in1=x_t[:, m, :], op0=ALU.mult, op1=ALU.add),
                reads=[psb[po], xb[m]], writes=[xb[m]])

    def mixer(i, l):
        t0 = i * T
        lam_init = 0.8 - 0.6 * math.exp(-0.3 * l)

        def kv_prefix_load(hd):
            slot = hd % 2
            if i > 0:
                P.op("sp", lambda e: e.dma_start(out=kpre[slot][:, 0:T * i], in_=kcache[l, hd][:, 0:T * i]),
                     reads=[kcacheb[l]], writes=kpreb[slot], dma=kpre_d[slot])
                P.op("sp", lambda e: e.dma_start(out=vpre[slot][:, 0:4 * i, :], in_=vcache[l, hd][:, 0:4 * i, :]),
                     reads=[vcacheb[l]], writes=vpreb[slot], dma=vpre_d[slot])
        rmsnorm_h(l, SP_MIX)
        wtq, wbq = wload(s_qkvc[l, 0], v3(8, 512), ("qkvc", l))
        wtk, wbk = wload(s_qkvc[l, 1], v3(8, 512), ("qkvc", l))
        pzq = []
        for hd in range(4):
            pzq.append(nps())
            held.add(pzq[hd])
        for c in range(NCH):
            for hd in range(4):
                P.op("pe", lambda e, c=c, hd=hd: e.matmul(ps_t[pzq[hd]][:], wtq[:, c, hd * 128:(hd + 1) * 128], h_t[:, c, :],
                                                          start=(c == 0), stop=(c == NCH - 1)),
                     reads=[wbq, hb[c]], writes=[psb[pzq[hd]]], signal=(c == NCH - 1))
        pzk = []
        for hd in range(4):
            pzk.append(nps())
            held.add(pzk[hd])
            proj8(pzk[hd], wtk, wbk, hd * 128)
            group_norm64(l, pzq[hd], [q_t[0:64, 0, hd, :], q_t[64:128, 1, hd, :]], SP_DAQ, [qb[hd]])
            held.discard(pzq[hd])
        for hd in range(4):
            group_norm64(l, pzk[hd], kcur[:, hd, :], SP_DAK, [kcurh[hd]])
            held.discard(pzk[hd])
        if i < NT - 1:
            P.op("pool", lambda e: e.dma_start(out=kcache[l][:, :, t0:t0 + T].rearrange("h p t -> p h t"), in_=kcur[:]),
                 reads=kcurh + [kcurb], writes=[kcacheb[l], kcurb], dma=kcurb)
        wt, wb = wload(s_qkvc[l, 2], v3(8, 512), ("qkvc", l))
        for tt in range(4):
            pv = nps()
            for c in range(NCH):
                P.op("pe", lambda e, c=c, pv=pv, tt=tt, wt=wt: e.matmul(
                    ps_t[pv][:], h_t[:, c, tt * 128:(tt + 1) * 128], wt[:, c, :], start=(c == 0), stop=(c == NCH - 1)),
                    reads=[wb, hb[c]], writes=[psb[pv]], signal=(c == NCH - 1))
            P.op("act", lambda e, pv=pv, tt=tt: e.activation(out=vcur[:, tt, :], in_=ps_t[pv][:], func=AF.Copy),
                 reads=[psb[pv]], writes=[vcurb])
        if i < NT - 1:
            for hd in range(4):
                P.op("pool", lambda e, hd=hd: e.dma_start(out=vcache[l, hd][:, 4 * i:4 * i + 4, :],
                                                          in_=vcur[:, :, hd * 128:(hd + 1) * 128]),
                     reads=[vcurb], writes=[vcacheb[l]], dma=vcurb)
        wt, wb = wload(s_pool[l], v3(8, 256), ("qkvc", l))
        if i == 0:
            P.op("pool", lambda e: e.memset(pbuf[:, :, 0:PH], 0.0), writes=[pbufb])
        else:
            P.op("pool", lambda e: e.tensor_copy(out=pbuf[:, :, 0:PH], in_=phalo[:, l, :, :]),
                 reads=[phalob], writes=[pbufb])
        for ch in range(2):
            pz = nps()
            proj8(pz, wt, wb, ch * 128)
            P.op("act", lambda e, pz=pz, ch=ch: e.activation(out=pbuf[:, ch, PH:PH + T], in_=ps_t[pz][:], func=AF.Copy),
                 reads=[psb[pz]], writes=[pbufb])
        W = PH + T
        P.op("pool", lambda e: e.tensor_tensor(out=wA[:, :, 1:W], in0=pbuf[:, :, 1:W], in1=pbuf[:, :, 0:W - 1], op=ALU.add),
             reads=[pbufb], writes=wAb)
        P.op("pool", lambda e: e.tensor_tensor(out=wB[:, :, 3:W], in0=wA[:, :, 3:W], in1=wA[:, :, 1:W - 2], op=ALU.add),
             reads=wAb, writes=wBb)
        P.op("pool", lambda e: e.tensor_tensor(out=wA[:, 1, 7:W], in0=wB[:, 1, 7:W], in1=wB[:, 1, 3:W - 4], op=ALU.add),
             reads=[wBb[1]], writes=[wAb[1]])
        P.op("pool", lambda e: e.tensor_tensor(out=wB[:, 1, 15:W], in0=wA[:, 1, 15:W], in1=wA[:, 1, 7:W - 8], op=ALU.add),
             reads=[wAb[1]], writes=[wBb[1]])
        P.op("pool", lambda e: e.tensor_copy(out=phalo[:, l, :, :], in_=pbuf[:, :, T:T + PH]),
             reads=[pbufb], writes=[phalob])
        for ch in range(2):
            for half in range(2):
                src = wA if half == 0 else wB
                srcb = wAb[ch] if half == 0 else wBb[ch]
                ps_ = slice(half * 64, (half + 1) * 64)
                P.op("dve", lambda e, ch=ch, src=src, ps_=ps_: e.scalar_tensor_tensor(
                    out=pp_t[ps_, ch, :], in0=src[ps_, ch, PH:W], scalar=cst[ps_, C_RW + ch:C_RW + ch + 1],
                    in1=pbuf[ps_, ch, PH:W], op0=ALU.mult, op1=ALU.subtract),
                    reads=[srcb, pbufb, cst_b], writes=[ppb[ch]])
                if i == 0:
                    ti = nxt("tmp", NTMP)
                    P.op("dve", lambda e, ch=ch, src=src, ps_=ps_, ti=ti: e.tensor_tensor(
                        out=tmp_t[ti][ps_, 0:16], in0=src[ps_, ch, PH:PH + 16],
                        in1=cst[ps_, C_RC0 + ch * 16:C_RC0 + (ch + 1) * 16], op=ALU.mult),
                        reads=[srcb, cst_b], writes=[tmpb[ti]])
                    P.op("dve", lambda e, ch=ch, ps_=ps_, ti=ti: e.tensor_tensor(
                        out=pp_t[ps_, ch, 0:16], in0=tmp_t[ti][ps_, 0:16], in1=pbuf[ps_, ch, PH:PH + 16], op=ALU.subtract),
                        reads=[tmpb[ti], pbufb], writes=[ppb[ch]])
        wt, wb = wload(s_qkvc[l, 3], v3(8, 512), ("qkvc", l))
        kv_prefix_load(0)
        kv_prefix_load(1)
        if i == 0:
            P.op("pool", lambda e: e.memset(cbuf[:, :, 0:CHL], 0.0), writes=cbufb)
        else:
            P.op("pool", lambda e: e.tensor_copy(out=cbuf[:, :, 0:CHL], in_=chalo[:, l, :, :]),
                 reads=[chalob], writes=cbufb)
        for ch in range(2):
            pa = nps()
            pg = nps()
            proj8(pa, wt, wb, ch * 128)
            proj8(pg, wt, wb, (2 + ch) * 128)
            ti = nxt("tmp", NTMP)
            P.op("act", lambda e, ti=ti, pg=pg: e.activation(out=tmp_t[ti][:], in_=ps_t[pg][:], func=AF.Sigmoid),
                 reads=[psb[pg]], writes=[tmpb[ti]])
            P.op("dve", lambda e, ti=ti, pa=pa, ch=ch: e.tensor_tensor(out=cbuf[:, ch, CHL:CHL + T], in0=ps_t[pa][:],
                                                                       in1=tmp_t[ti][:], op=ALU.mult),
                 reads=[psb[pa], tmpb[ti]], writes=[cbufb[ch]])
        P.op("pool", lambda e: e.tensor_copy(out=chalo[:, l, :, :], in_=cbuf[:, :, T:T + CHL]),
             reads=cbufb, writes=[chalob])
        def conv_tap_ops(j0, j1):
            ops = []
            for j in range(j0, j1):
                for ch in range(2):
                    wcol = sp_col(l, SP_CDW + ch * 31 + j)
                    src = cbuf[:, ch, 2 + j:2 + j + T]
                    if j == 0:
                        ops.append(lambda ch=ch, wcol=wcol, src=src: P.op("dve", lambda e: e.tensor_scalar(
                            out=cacc[:, ch, :], in0=src, scalar1=wcol, scalar2=sp_col(l, SP_CDB + ch),
                            op0=ALU.mult, op1=ALU.add), reads=[cbufb[ch], spt_b], writes=[caccb[ch]]))
                    else:
                        ops.append(lambda ch=ch, wcol=wcol, src=src: P.op("dve", lambda e: e.scalar_tensor_tensor(
                            out=cacc[:, ch, :], in0=src, scalar=wcol, in1=cacc[:, ch, :], op0=ALU.mult, op1=ALU.add),
                            reads=[cbufb[ch], caccb[ch], spt_b], writes=[caccb[ch]]))
            return ops

        ps_limit[0] = 4
        nkt = 4 * (i + 1)
        tap_sched = {0: (0, 8), 1: (8, 16), 2: (16, 24), 3: (24, 31)}
        LOOK = 3
        deferred = []

        def tick():
            for d in deferred:
                d[0] -= 1
            deferred.sort(key=lambda d: d[0])
            while deferred and deferred[0][0] <= 0:
                deferred.pop(0)[1]()
                deferred.sort(key=lambda d: d[0])
        for hd in range(4):
            slot = hd % 2
            tap_ops = conv_tap_ops(*tap_sched[hd])
            taps_per_step = -(-len(tap_ops) // (8 * (i + 1)))
            bo = (4, 5)
            bd = (6, 7)
            steps = [(kt, c) for kt in range(nkt) for c in range(2)]
            st_pi = {}

            def emit_S(kt, c, hd=hd, slot=slot, st_pi=st_pi):
                j = kt - 4 * i
                q0 = 128 * j if j > 0 else 0
                pS = nps()
                if j < 0:
                    lk = kpre[slot][:, kt * 128:(kt + 1) * 128]
                    lkb = kpreb[slot]
                else:
                    lk = kcur[:, hd, j * 128:(j + 1) * 128]
                    lkb = [kcurh[hd]]
                qv = q_t[:, c, hd, q0:T]
                P.op("pe", lambda e: e.matmul(ps_t[pS][:, q0:T], lk, qv, start=True, stop=True),
                     reads=lkb + [qb[hd]], writes=[psb[pS]])
                pi = nxt("pt", NPT)
                st_pi[(kt, c)] = pi
                P.op("act", lambda e: e.activation(out=pt_t[pi][:, q0:T], in_=ps_t[pS][:, q0:T], func=AF.Exp, scale=0.125),
                     reads=[psb[pS]], writes=[ptb[pi]])
                if j >= 0:
                    P.op("dve", lambda e: e.tensor_tensor(
                        out=pt_t[pi][:, q0:q0 + 128], in0=pt_t[pi][:, q0:q0 + 128], in1=cstb[:, C_TRI:C_TRI + 128], op=ALU.mult),
                        reads=[ptb[pi], cstb_b], writes=[ptb[pi]])

            def emit_PV(kt, c, hd=hd, slot=slot, st_pi=st_pi):
                j = kt - 4 * i
                q0 = 128 * j if j > 0 else 0
                pi = st_pi[(kt, c)]
                if j < 0:
                    lv = vpre[slot][:, kt, :]
                    lvb = vpreb[slot]
                else:
                    lv = vcur[:, j, hd * 128:(hd + 1) * 128]
                    lvb = [vcurb]
                ob, db = bo[c], bd[c]
                P.op("pe", lambda e: e.matmul(ps_t[ob][:, q0:T], lv, pt_t[pi][:, q0:T], start=(kt == 0), stop=(kt == nkt - 1)),
                     reads=lvb + [ptb[pi]], writes=[psb[ob]], signal=False)
                P.op("pe", lambda e: e.matmul(ps_t[db][:, q0:T], cstb[:, C_ONES:C_ONES + 128], pt_t[pi][:, q0:T],
                                              start=(kt == 0), stop=(kt == nkt - 1)),
                     reads=[cstb_b, ptb[pi]], writes=[psb[db]], signal=(kt == nkt - 1))

            for sidx in range(len(steps) + LOOK):
                if sidx < len(steps):
                    emit_S(*steps[sidx])
                if sidx - LOOK >= 0:
                    emit_PV(*steps[sidx - LOOK])
                tick()
                for _ in range(min(len(tap_ops), taps_per_step)):
                    tap_ops.pop(0)()
            while tap_ops:
                tap_ops.pop(0)()
            ta, tb, tc, td = (nxt("tmp", NTMP) for _ in range(4))
            for tx, c in ((ta, 0), (tb, 1)):
                P.op("act", lambda e, tx=tx, c=c: e.activation(out=tmp_t[tx][:], in_=ps_t[bd[c]][:], func=AF.Ln),
                     reads=[psb[bd[c]]], writes=[tmpb[tx]])
            for tx, c in ((tc, 0), (td, 1)):
                P.op("dve", lambda e, tx=tx, c=c: e.tensor_copy(out=tmp_t[tx][:], in_=ps_t[bo[c]][:]),
                     reads=[psb[bo[c]]], writes=[tmpb[tx]])
            if hd + 2 < 4:
                kv_prefix_load(hd + 2)
            si = nxt("sq", NSQ)

            def finalize_a2(ta=ta, tb=tb, tc=tc, td=td, si=si):
                for tx in (ta, tb):
                    P.op("act", lambda e, tx=tx: e.activation(out=tmp_t[tx][:], in_=tmp_t[tx][:], func=AF.Exp, scale=-1.0),
                         reads=[tmpb[tx]], writes=[tmpb[tx]])
                for to, tr in ((tc, ta), (td, tb)):
                    P.op("dve", lambda e, to=to, tr=tr: e.tensor_tensor(out=tmp_t[to][:], in0=tmp_t[to][:], in1=tmp_t[tr][:], op=ALU.mult),
                         reads=[tmpb[to], tmpb[tr]], writes=[tmpb[to]])
                P.op("dve", lambda e: e.scalar_tensor_tensor(out=tmp_t[tc][:], in0=tmp_t[td][:], scalar=neglam[:, l:l + 1],
                                                             in1=tmp_t[tc][:], op0=ALU.mult, op1=ALU.add),
                     reads=[tmpb[tc], tmpb[td], lam_b], writes=[tmpb[tc]])

                def finalize_a3():
                    P.op("act", lambda e: e.activation(out=sq_t[si][:], in_=tmp_t[tc][:], func=AF.Square,
                                                       scale=1.0 / (1.0 - lam_init)),
                         reads=[tmpb[tc]], writes=[sqb[si]])
                deferred.append([3, finalize_a3])
            deferred.append([3, finalize_a2])

            def finalize_b(hd=hd, tc=tc, si=si):
                p2 = nps()
                P.op("pe", lambda e: e.matmul(ps_t[p2][:], onesd[:, 2, :], sq_t[si][:], start=True, stop=True),
                     reads=[sqb[si], cstb_b], writes=[psb[p2]])
                ti = rstd_from_ps(p2, EPS / (1.0 - lam_init) ** 2)
                P.op("dve", lambda e: e.scalar_tensor_tensor(
                    out=oa_t[:, hd, :], in0=tmp_t[tc][:], scalar=sp_col(l, SP_SUBLN), in1=tmp_t[ti][:], op0=ALU.mult, op1=ALU.mult),
                    reads=[tmpb[tc], tmpb[ti], spt_b], writes=[oab[hd]])
            deferred.append([10, finalize_b])
        while deferred:
            deferred.sort(key=lambda d: d[0])
            deferred.pop(0)[1]()
        ps_limit[0] = 8
        for ch in range(2):
            pz = nps()
            P.op("pe", lambda e, pz=pz, ch=ch: e.matmul(ps_t[pz][:], pwbd[:, l, ch, :], pp_t[:, ch, :], start=True, stop=True),
                 reads=[pwbdb, ppb[ch]], writes=[psb[pz]])
            P.op("act", lambda e, pz=pz, ch=ch: e.activation(out=ypool[:, ch, :], in_=ps_t[pz][:], func=AF.Identity,
                                                             scale=sp_col(l, SP_PSCALE + ch)),
                 reads=[psb[pz], spt_b], writes=[ypoolb[ch]])
        pm = nps()
        for ch in range(2):
            si = nxt("sq", NSQ)
            P.op("act", lambda e, ch=ch, si=si: e.activation(out=sq_t[si][:], in_=cacc[:, ch, :], func=AF.Copy),
                 reads=[caccb[ch]], writes=[sqb[si]])
            P.op("pe", lambda e, ch=ch, pm=pm, si=si: e.matmul(ps_t[pm][:], onesd[:, 3, :], sq_t[si][:], start=(ch == 0), stop=(ch == 1)),
                 reads=[sqb[si], cstb_b], writes=[psb[pm]], signal=True)
        pvv = nps()
        for ch in range(2):
            P.op("dve", lambda e, ch=ch, pm=pm: e.tensor_tensor(out=cacc[:, ch, :], in0=cacc[:, ch, :], in1=ps_t[pm][:], op=ALU.subtract),
                 reads=[caccb[ch], psb[pm]], writes=[caccb[ch]])
            si = nxt("sq", NSQ)
            P.op("act", lambda e, ch=ch, si=si: e.activation(out=sq_t[si][:], in_=cacc[:, ch, :], func=AF.Square),
                 reads=[caccb[ch]], writes=[sqb[si]])
            P.op("pe", lambda e, ch=ch, si=si, pvv=pvv: e.matmul(ps_t[pvv][:], onesd[:, 3, :], sq_t[si][:], start=(ch == 0), stop=(ch == 1)),
                 reads=[sqb[si], cstb_b], writes=[psb[pvv]], signal=True)
        ti = rstd_from_ps(pvv, EPS)
        for ch in range(2):
            P.op("dve", lambda e, ch=ch, ti=ti: e.tensor_tensor(out=cacc[:, ch, :], in0=cacc[:, ch, :], in1=tmp_t[ti][:], op=ALU.mult),
                 reads=[caccb[ch], tmpb[ti]], writes=[caccb[ch]])
            P.op("act", lambda e, ch=ch: e.activation(out=yconv[:, ch, :], in_=cacc[:, ch, :], func=AF.Silu,
                                                      scale=sp_col(l, SP_CLNG + ch), bias=sp_col(l, SP_CLNB + ch)),
                 reads=[caccb[ch], spt_b], writes=[yconvb[ch]])
        for m in range(8):
            wt, wb = wload(s_mrg[l, m], v3(8, 512), ("mrg", l))
            tg = []
            ysrc = ((0, 4, oa_t, oab), (4, 2, ypool, ypoolb), (6, 2, yconv, yconvb))
            for g in range(3):
                pg = nps()
                proj8(pg, wt, wb, 128 + g * 128)
                py = nps()
                c0, ncg, ysb, ybufs = ysrc[g]
                for c in range(ncg):
                    P.op("pe", lambda e, c=c, c0=c0, ncg=ncg, ysb=ysb, py=py, wt=wt: e.matmul(
                        ps_t[py][:], wt[:, c0 + c, 0:128], ysb[:, c, :], start=(c == 0), stop=(c == ncg - 1)),
                        reads=[wb, ybufs[c]], writes=[psb[py]], signal=(c == ncg - 1))
                ti = nxt("tmp", NTMP)
                tg.append(ti)
                P.op("act", lambda e, ti=ti, pg=pg, g=g: e.activation(out=tmp_t[ti][:], in_=ps_t[pg][:], func=AF.Sigmoid,
                                                                     bias=sp_col(l, SP_BGATE + g * 8 + m)),
                     reads=[psb[pg], spt_b], writes=[tmpb[ti]])
                P.op("dve", lambda e, ti=ti, py=py: e.tensor_tensor(out=tmp_t[ti][:], in0=ps_t[py][:], in1=tmp_t[ti][:], op=ALU.mult),
                     reads=[psb[py], tmpb[ti]], writes=[tmpb[ti]])
            P.op("dve", lambda e, a=tg[0], b=tg[1]: e.tensor_tensor(out=tmp_t[a][:], in0=tmp_t[a][:], in1=tmp_t[b][:], op=ALU.add),
                 reads=[tmpb[tg[0]], tmpb[tg[1]]], writes=[tmpb[tg[0]]])
            P.op("dve", lambda e, a=tg[0], b=tg[2], m=m: e.tensor_tensor(out=mrg_t[:, m, :], in0=tmp_t[a][:], in1=tmp_t[b][:], op=ALU.add),
                 reads=[tmpb[tg[0]], tmpb[tg[2]]], writes=[mrgb[m]])
        P.op("act", lambda e: e.activation(out=junk[:, 1:2], in_=epst[:, 0:1], func=AF.Ln), reads=[cstb_b], writes=[junkb])
        for s in range(2):
            wt, wb = wload(s_wout[l, s], v3(8, 512), ("mrg", l))
            for mm in range(4):
                mo = 4 * s + mm
                po = nps()
                for c in range(NCH):
                    P.op("pe", lambda e, c=c, po=po, mm=mm, wt=wt: e.matmul(
                        ps_t[po][:], wt[:, c, mm * 128:(mm + 1) * 128], mrg_t[:, c, :], start=(c == 0), stop=(c == NCH - 1)),
                        reads=[wb, mrgb[c]], writes=[psb[po]], signal=(c == NCH - 1))
                P.op("dve", lambda e, mo=mo, po=po: e.tensor_tensor(out=x_t[:, mo, :], in0=ps_t[po][:], in1=x_t[:, mo, :], op=ALU.add),
                     reads=[psb[po], xb[mo]], writes=[xb[mo]])

    def xattn(i, l):
        rmsnorm_h(l, SP_XA)
        wt, wb = wload(s_xaq[l], v3(8, 256), ("xa", l))
        pzx = [nps(), nps()]
        held.update(pzx)
        for c in range(NCH):
            for ch in range(2):
                P.op("pe", lambda e, c=c, ch=ch: e.matmul(ps_t[pzx[ch]][:], wt[:, c, ch * 128:(ch + 1) * 128], h_t[:, c, :],
                                                          start=(c == 0), stop=(c == NCH - 1)),
                     reads=[wb, hb[c]], writes=[psb[pzx[ch]]], signal=(c == NCH - 1))
        for ch in range(2):
            group_norm64(l, pzx[ch], [qx_t[0:64, 2 * ch, :], qx_t[64:128, 2 * ch + 1, :]], SP_XAQ, [qxb[2 * ch], qxb[2 * ch + 1]])
            held.discard(pzx[ch])
        wo, wob = wload(s_xao[l], v3(2, 1024), ("xa", l))
        ps_limit[0] = 4
        xsteps = [(hh, mt) for hh in range(4) for mt in range(2)]
        xpi = {}

        def x_S(hh, mt):
            ch = hh // 2
            pS = nps()
            P.op("pe", lambda e: e.matmul(ps_t[pS][:], kmem[:, l, ch, mt * 128:(mt + 1) * 128], qx_t[:, hh, :], start=True, stop=True),
                 reads=[kmemb, qxb[hh]], writes=[psb[pS]])
            pi = nxt("pt", NPT)
            xpi[(hh, mt)] = pi
            P.op("act", lambda e: e.activation(out=pt_t[pi][:], in_=ps_t[pS][:], func=AF.Exp, scale=0.125),
                 reads=[psb[pS]], writes=[ptb[pi]])

        def x_PV(hh, mt):
            ch, half = hh // 2, hh % 2
            hs = slice(half * 64, (half + 1) * 64)
            bo, bd = (4, 5) if hh % 2 == 0 else (6, 7)
            pi = xpi[(hh, mt)]
            P.op("pe", lambda e: e.matmul(ps_t[bo][:], vmem[:, l, mt, ch * 128:(ch + 1) * 128], pt_t[pi][:], start=(mt == 0), stop=(mt == 1)),
                 reads=[vmemb, ptb[pi]], writes=[psb[bo]], signal=False)
            P.op("pe", lambda e: e.matmul(ps_t[bd][:], cstb[:, C_ONES:C_ONES + 128], pt_t[pi][:], start=(mt == 0), stop=(mt == 1)),
                 reads=[cstb_b, ptb[pi]], writes=[psb[bd]], signal=(mt == 1))
            if mt == 1:
                ta = recip_from_ps(bd)
                P.op("dve", lambda e: e.tensor_tensor(out=ox_t[hs, ch, :], in0=ps_t[bo][hs, :], in1=tmp_t[ta][hs, :], op=ALU.mult),
                     reads=[psb[bo], tmpb[ta]], writes=[oxb[hh]])

        XL = 3
        for sidx in range(len(xsteps) + XL):
            if sidx < len(xsteps):
                x_S(*xsteps[sidx])
            if sidx - XL >= 0:
                x_PV(*xsteps[sidx - XL])
        ps_limit[0] = 8
        for mo in range(8):
            po = nps()
            for ch in range(2):
                P.op("pe", lambda e, ch=ch, po=po, mo=mo: e.matmul(
                    ps_t[po][:], wo[:, ch, mo * 128:(mo + 1) * 128], ox_t[:, ch, :], start=(ch == 0), stop=(ch == 1)),
                    reads=[wob, oxb[2 * ch], oxb[2 * ch + 1]], writes=[psb[po]], signal=(ch == 1))
            P.op("dve", lambda e, mo=mo, po=po: e.tensor_tensor(out=x_t[:, mo, :], in0=ps_t[po][:], in1=x_t[:, mo, :], op=ALU.add),
                 reads=[psb[po], xb[mo]], writes=[xb[mo]])

    def mem_prologue(l):
        n = NMEM
        rmsnorm_h(l, SP_MEMN, n=n)
        wt, wb = wload(s_xakv[l], v3(8, 512), ("xa", l))
        for ch in range(2):
            pz = nps()
            proj8(pz, wt, wb, ch * 128, n=n)
            group_norm64(l, pz, kmem[:, l, ch, :], SP_XAK, [kmemb], n=n)
        for mt in range(2):
            pv = nps()
            for c in range(NCH):
                P.op("pe", lambda e, c=c, pv=pv, mt=mt, wt=wt: e.matmul(
                    ps_t[pv][:, 0:256], h_t[:, c, mt * 128:(mt + 1) * 128], wt[:, c, 256:512], start=(c == 0), stop=(c == NCH - 1)),
                    reads=[wb, hb[c]], writes=[psb[pv]], signal=(c == NCH - 1))
            P.op("act", lambda e, pv=pv, mt=mt: e.activation(out=vmem[:, l, mt, :], in_=ps_t[pv][:, 0:256], func=AF.Copy),
                 reads=[psb[pv]], writes=[vmemb])

    conv_sched = []
    for l in layers:
        if "ffn1" in stages:
            conv_sched.append(lambda l=l: None)
        if "mix" in stages:
            conv_sched.append(lambda l=l: convert_mix(l))
        if "ffn2" in stages:
            conv_sched.append(lambda l=l: None)
    conv_pos = [0]

    def conv_ahead(upto):
        while conv_pos[0] < min(upto, len(conv_sched)):
            conv_sched[conv_pos[0]]()
            conv_pos[0] += 1

    if "xa" in stages:
        for l in layers:
            convert_xa(l)
    conv_ahead(2)

    if "mix" in stages:
        P.op("pool", lambda e: e.memset(pwbd[:], 0.0), writes=[pwbdb])
        for l in layers:
            for g in range(4):
                gs = slice((g % 2) * 64, (g % 2 + 1) * 64)
                P.op("pool", lambda e, l=l, g=g, gs=gs: e.dma_start(out=pwbd[gs, l, g // 2, gs], in_=pool_w[l, g]),
                     writes=[pwbdb], dma=pwbdb)
            lam_init = 0.8 - 0.6 * math.exp(-0.3 * l)
            t_l, t_p, t_s = 0, 1, 2
            P.op("sp", lambda e, l=l: e.dma_start(out=tmp_t[t_l][:, 0:256], in_=lamb[l]), writes=[tmpb[t_l]], dma=tmpb[t_l])
            for k in range(2):
                P.op("dve", lambda e, k=k: e.tensor_tensor(out=tmp_t[t_p][:, k * 64:(k + 1) * 64], in0=tmp_t[t_l][:, k * 128:k * 128 + 64],
                                                           in1=tmp_t[t_l][:, k * 128 + 64:k * 128 + 128], op=ALU.mult),
                     reads=[tmpb[t_l]], writes=[tmpb[t_p]])
                P.op("dve", lambda e, k=k: e.reduce_sum(out=tmp_t[t_s][:, k:k + 1], in_=tmp_t[t_p][:, k * 64:(k + 1) * 64],
                                                        axis=mybir.AxisListType.X),
                     reads=[tmpb[t_p]], writes=[tmpb[t_s]])
            P.op("act", lambda e: e.activation(out=tmp_t[t_s][:, 2:4], in_=tmp_t[t_s][:, 0:2], func=AF.Exp),
                 reads=[tmpb[t_s]], writes=[tmpb[t_s]])
            P.op("dve", lambda e, l=l, lam_init=lam_init: e.scalar_tensor_tensor(
                out=neglam[:, l:l + 1], in0=tmp_t[t_s][:, 3:4], scalar=-lam_init, in1=tmp_t[t_s][:, 2:3],
                op0=ALU.add, op1=ALU.subtract), reads=[tmpb[t_s]], writes=[lam_b])
    if "xa" in stages:
        P.op("sp", lambda e: e.dma_start(out=x_t[:, :, 0:NMEM], in_=memT.rearrange("(c p) t -> p c t", p=128)),
             writes=xb + [x_dma], dma=x_dma)
        for l in layers:
            mem_prologue(l)

    stage_no = [0]
    for i in range(ntiles):
        t0 = i * T
        P.op("sp", lambda e, t0=t0: e.dma_start(out=x_t[:], in_=xT[:, t0:t0 + T].rearrange("(c p) t -> p c t", p=128)),
             writes=xb + [x_dma], dma=x_dma)
        for l in layers:
            if "ffn1" in stages:
                stage_no[0] += 1
                conv_ahead(stage_no[0] + 2)
                ffn(0, l, direct=(i == 0))
            if "mix" in stages:
                stage_no[0] += 1
                conv_ahead(stage_no[0] + 2)
                mixer(i, l)
            if "xa" in stages:
                xattn(i, l)
            if "ffn2" in stages:
                stage_no[0] += 1
                conv_ahead(stage_no[0] + 2)
                ffn(1, l, direct=(i == 0))
        P.op("sp", lambda e, t0=t0: e.dma_start(out=yT[:, t0:t0 + T].rearrange("(c p) t -> p c t", p=128), in_=x_t[:]),
             reads=xb, writes=[x_dma], dma=x_dma)
    P.op("sp", None, reads=[x_dma], writes=[x_dma], signal=False)
    P.engs["sp"].pending = []

    P.check()
    P.emit(nc, st)
    st.close()
    return nc, P


def _col(v, n):
    return np.ascontiguousarray(np.asarray(v, np.float32).reshape(n, 128).T)


def pack_spar(inp):
    sp = np.zeros((DEPTH, 128, NSP), np.float32)
    for l in range(DEPTH):
        sp[l, :, SP_FFN1:SP_FFN1 + 8] = _col(inp["ffn1_norm"][l], 8)
        sp[l, :, SP_MIX:SP_MIX + 8] = _col(inp["mix_norm"][l], 8)
        sp[l, :, SP_XA:SP_XA + 8] = _col(inp["xa_norm"][l], 8)
        sp[l, :, SP_MEMN:SP_MEMN + 8] = _col(inp["xa_mem_norm"][l], 8)
        sp[l, :, SP_FFN2:SP_FFN2 + 8] = _col(inp["ffn2_norm"][l], 8)
        sp[l, :, SP_BGATE:SP_BGATE + 24] = _col(inp["b_gate"][l], 24)
        sp[l, :, SP_DAQ] = np.tile(np.asarray(inp["da_q_norm"][l], np.float32), 2)
        sp[l, :, SP_DAK] = np.tile(np.asarray(inp["da_k_norm"][l], np.float32), 2)
        sp[l, :, SP_SUBLN] = np.asarray(inp["da_subln"][l], np.float32)
        sp[l, :, SP_PSCALE:SP_PSCALE + 2] = _col(inp["pool_scale"][l], 2)
        sp[l, :, SP_CDB:SP_CDB + 2] = _col(inp["conv_db"][l], 2)
        sp[l, :, SP_CLNG:SP_CLNG + 2] = _col(inp["conv_ln_g"][l], 2)
        sp[l, :, SP_CLNB:SP_CLNB + 2] = _col(inp["conv_ln_b"][l], 2)
        sp[l, :, SP_XAQ] = np.tile(np.asarray(inp["xa_q_norm"][l], np.float32), 2)
        sp[l, :, SP_XAK] = np.tile(np.asarray(inp["xa_k_norm"][l], np.float32), 2)
        cdw = np.asarray(inp["conv_dw"][l], np.float32)
        sp[l, :, SP_CDW:SP_CDW + 62] = cdw.T.reshape(2, 128, 31).transpose(1, 0, 2).reshape(128, 62)
    return sp


def make_consts():
    c = np.zeros((128, NCONST), np.float32)
    c[:, C_ONES:C_ONES + 128] = 1.0
    for g in range(2):
        c[g * 64:(g + 1) * 64, C_BLK64 + g * 64:C_BLK64 + (g + 1) * 64] = 1.0
    k = np.arange(128)[:, None]
    q = np.arange(128)[None, :]
    c[:, C_TRI:C_TRI + 128] = (q >= k).astype(np.float32)
    wins = (2, 4, 8, 16)
    for ch in range(2):
        for half in range(2):
            w = wins[2 * ch + half]
            c[half * 64:(half + 1) * 64, C_RW + ch] = 1.0 / w
            t = np.arange(16)
            c[half * 64:(half + 1) * 64, C_RC0 + ch * 16:C_RC0 + (ch + 1) * 16] = 1.0 / np.minimum(t + 1, w)
    return c


_CACHE = {}


def kernel(**inputs):
    inp = {k: np.asarray(v) for k, v in inputs.items()}
    if "nc" not in _CACHE:
        _CACHE["nc"] = build()[0]
    nc = _CACHE["nc"]
    x = inp["x"].astype(np.float32, copy=False)
    mem = inp["mem"].astype(np.float32, copy=False)
    spar = pack_spar(inp)
    consts = make_consts()
    lamb = np.ascontiguousarray(np.broadcast_to(
        np.asarray(inp["da_lambda"], np.float32).reshape(DEPTH, 1, 256), (DEPTH, 128, 256)))
    shared = {
        "spar": spar, "consts": consts, "lamb": lamb,
        "ffn1_w_gu": inp["ffn1_w_gu"], "ffn2_w_gu": inp["ffn2_w_gu"],
        "ffn1_w_down": inp["ffn1_w_down"], "ffn2_w_down": inp["ffn2_w_down"],
        "w_in": inp["w_in"], "w_proj_attn": inp["w_proj_attn"], "w_proj_pool": inp["w_proj_pool"],
        "w_proj_conv": inp["w_proj_conv"], "w_out": inp["w_out"], "pool_w": inp["pool_w"],
        "xa_w_q": inp["xa_w_q"], "xa_w_kv": inp["xa_w_kv"], "xa_w_o": inp["xa_w_o"],
    }
    in_maps = []
    for b in range(NCORES):
        m = dict(shared)
        m["xT"] = np.ascontiguousarray(x[b].T)
        m["memT"] = np.ascontiguousarray(mem[b].T)
        in_maps.append(m)
    res = run_bass_kernel_spmd(nc, in_maps, core_ids=list(range(NCORES)))
    out = np.stack([np.ascontiguousarray(r["yT"].T) for r in res.results], axis=0)
    return out.astype(np.float32, copy=False)
```
